# Optimizing a Trainium2 kernel written in Bass

```python
import math
import numpy as np
import jax
import jax.numpy as jnp
from jax import lax

D_MODEL = 1024
BATCH = 4
SEQ = 4096
DEPTH = 2

HEAD_DIM = 64
ROT_DIM = HEAD_DIM // 4
ROPE_THETA = 500000.0
RMS_EPS = 1e-6
MASK_VALUE = -1e30

A_HEADS = 4
A_VDIM = 2 * HEAD_DIM
A_QBLOCK = 128
A_WIDTH = A_HEADS * A_VDIM

B_PATTERNS = ((128, 1), (512, 4), (2048, 16))
B_GROUPS = 3
B_HEADS = 4
B_BAND = 64
B_WIDTH = B_HEADS * HEAD_DIM

GRID_W = 64
C_HEADS = 4
NA_KH = 8
NA_KW = 16
NA_QB = 16
NA_KSPAN = 2 * NA_KW
C_WIDTH = C_HEADS * HEAD_DIM

N_BRANCHES = 3
BR_WIDTH = A_WIDTH + B_WIDTH + C_WIDTH
A_QK_COLS = A_HEADS * 2 * HEAD_DIM
B_COLS = B_GROUPS * B_HEADS * HEAD_DIM
IN_SPLITS = (A_QK_COLS, A_QK_COLS, A_WIDTH, B_COLS, B_COLS, B_COLS, C_WIDTH, C_WIDTH, C_WIDTH, BR_WIDTH, N_BRANCHES * D_MODEL)
IN_COLS = 2 * A_QK_COLS + A_WIDTH + 3 * B_COLS + 3 * C_WIDTH + BR_WIDTH + N_BRANCHES * D_MODEL

kernel_name = 'hybrid_gated_mixer_encoder'


def rms_norm(x, gain):
    xf = x.astype(jnp.float32)
    y = xf * lax.rsqrt(jnp.mean(xf * xf, axis=-1, keepdims=True) + RMS_EPS)
    return (y * gain.astype(jnp.float32)).astype(x.dtype)


def rope_tables(positions):
    inv = np.float32(ROPE_THETA) ** (-np.arange(0, ROT_DIM, 2, dtype=np.float32) / np.float32(ROT_DIM))
    ang = positions.astype(jnp.float32)[..., None] * jnp.asarray(inv, dtype=jnp.float32)
    return jnp.cos(ang), jnp.sin(ang)


def apply_partial_rope(t, cos, sin):
    shp = cos.shape[:2] + (1,) * (t.ndim - 3) + cos.shape[-1:]
    cos = cos.reshape(shp).astype(t.dtype)
    sin = sin.reshape(shp).astype(t.dtype)
    half = ROT_DIM // 2
    t1, t2, rest = t[..., :half], t[..., half:ROT_DIM], t[..., ROT_DIM:]
    return jnp.concatenate([t1 * cos - t2 * sin, t2 * cos + t1 * sin, rest], axis=-1)


def split_cols(t, sizes):
    idx, acc = [], 0
    for s in sizes[:-1]:
        acc += s
        idx.append(acc)
    return jnp.split(t, idx, axis=-1)


def diff_attention(q, k, v, lam, lambda_init, subln_gain):
    bn, s_len = q.shape[:2]
    nq = s_len // A_QBLOCK
    scale = HEAD_DIM ** -0.5
    qb = jnp.moveaxis(q.reshape(bn, nq, A_QBLOCK, A_HEADS, 2, HEAD_DIM), 1, 0)

    def one_block(qi):
        s = jnp.einsum('bqhcd,bkhcd->bhcqk', qi, k).astype(jnp.float32) * scale
        p = jax.nn.softmax(s, axis=-1)
        a = (p[:, :, 0] - lam * p[:, :, 1]).astype(v.dtype)
        return jnp.einsum('bhqk,bkhe->bqhe', a, v)

    o = lax.map(one_block, qb)
    o = jnp.moveaxis(o, 0, 1).reshape(bn, s_len, A_HEADS, A_VDIM)
    o = rms_norm(o, subln_gain) * (1.0 - lambda_init)
    return o.reshape(bn, s_len, A_WIDTH)


def dilated_group(q, k, v, dilation, radius):
    bn, s_len, h, dh = q.shape
    L = s_len // dilation

    def to_sub(t):
        return jnp.moveaxis(t.reshape(bn, L, dilation, h, dh), 2, 1)

    qs, ks, vs = to_sub(q), to_sub(k), to_sub(v)
    nb = -(-L // B_BAND)
    lp = nb * B_BAND
    qb = jnp.pad(qs, ((0, 0), (0, 0), (0, lp - L), (0, 0), (0, 0))).reshape(bn, dilation, nb, B_BAND, h, dh)
    pad_kv = ((0, 0), (0, 0), (B_BAND, lp - L + B_BAND), (0, 0), (0, 0))
    kp = jnp.pad(ks, pad_kv).reshape(bn, dilation, nb + 2, B_BAND, h, dh)
    vp = jnp.pad(vs, pad_kv).reshape(bn, dilation, nb + 2, B_BAND, h, dh)
    kw = jnp.concatenate([kp[:, :, :-2], kp[:, :, 1:-1], kp[:, :, 2:]], axis=3)
    vw = jnp.concatenate([vp[:, :, :-2], vp[:, :, 1:-1], vp[:, :, 2:]], axis=3)
    s = jnp.einsum('bmnqhd,bmnkhd->bmnhqk', qb, kw).astype(jnp.float32) * (HEAD_DIM ** -0.5)
    qi = np.arange(nb)[:, None, None] * B_BAND + np.arange(B_BAND)[None, :, None]
    kj = np.arange(nb)[:, None, None] * B_BAND + np.arange(3 * B_BAND)[None, None, :] - B_BAND
    valid = (np.abs(kj - qi) <= radius) & (kj >= 0) & (kj < L)
    s = jnp.where(valid[None, None, :, None], s, MASK_VALUE)
    m = jnp.max(s, axis=-1, keepdims=True)
    e = jnp.exp(s - m)
    den = jnp.sum(e, axis=-1, keepdims=True)
    o = jnp.einsum('bmnhqk,bmnkhd->bmnqhd', (e / den).astype(v.dtype), vw)
    lse = jnp.moveaxis((m + jnp.log(den))[..., 0], 3, 4)
    o = o.reshape(bn, dilation, lp, h, dh)[:, :, :L]
    o = jnp.moveaxis(o, 1, 2).reshape(bn, s_len, h, dh)
    lse = lse.reshape(bn, dilation, lp, h)[:, :, :L]
    lse = jnp.moveaxis(lse, 1, 2).reshape(bn, s_len, h)
    return o, lse


def dilated_mixture(q, k, v):
    outs, lses = [], []
    for g, (window, dilation) in enumerate(B_PATTERNS):
        o, lse = dilated_group(q[:, :, g], k[:, :, g], v[:, :, g], dilation, (window // 2) // dilation)
        outs.append(o)
        lses.append(lse)
    alpha = jax.nn.softmax(jnp.stack(lses, axis=0), axis=0)
    o = jnp.einsum('gbsh,gbshd->bshd', alpha.astype(outs[0].dtype), jnp.stack(outs, axis=0))
    bn, s_len = q.shape[:2]
    return o.reshape(bn, s_len, B_WIDTH)


def neighborhood_attention(q, k, v, rpb):
    bn, s_len, h, dh = q.shape
    rows = s_len // GRID_W
    kh = min(NA_KH, rows)
    ncb = GRID_W // NA_QB
    r_idx = np.arange(rows)
    row_start = np.clip(r_idx - kh // 2, 0, rows - kh)
    row_idx = row_start[:, None] + np.arange(kh)[None, :]
    blk_start = np.clip(np.arange(ncb) * NA_QB - NA_KW // 2, 0, GRID_W - NA_KSPAN)
    col_idx = blk_start[:, None] + np.arange(NA_KSPAN)[None, :]
    qcol = np.arange(ncb)[:, None] * NA_QB + np.arange(NA_QB)[None, :]
    col_start = np.clip(qcol - NA_KW // 2, 0, GRID_W - NA_KW)
    col_valid = (col_idx[:, None, :] >= col_start[..., None]) & (col_idx[:, None, :] < col_start[..., None] + NA_KW)
    mask = np.broadcast_to(col_valid[:, :, None, :], (ncb, NA_QB, kh, NA_KSPAN)).reshape(ncb, NA_QB, kh * NA_KSPAN)

    def gather_kv(t):
        g = t.reshape(bn, rows, GRID_W, h, dh)[:, row_idx]
        g = g[:, :, :, col_idx]
        g = jnp.moveaxis(g, 3, 2)
        return g.reshape(bn, rows, ncb, kh * NA_KSPAN, h, dh)

    kg, vg = gather_kv(k), gather_kv(v)
    qb = q.reshape(bn, rows, ncb, NA_QB, h, dh)
    s = jnp.einsum('brnqhd,brnkhd->brnhqk', qb, kg).astype(jnp.float32) * (HEAD_DIM ** -0.5)
    dr_i = row_idx - r_idx[:, None] + (NA_KH - 1)
    dc_i = np.clip(col_idx[:, None, :] - qcol[..., None] + (NA_KW - 1), 0, 2 * NA_KW - 2)
    bias = rpb[:, dr_i[:, None, None, :, None], dc_i[None, :, :, None, :]]
    bias = jnp.transpose(bias, (1, 2, 0, 3, 4, 5)).reshape(rows, ncb, h, NA_QB, kh * NA_KSPAN)
    s = jnp.where(mask[None, None, :, None], s + bias.astype(jnp.float32), MASK_VALUE)
    p = jax.nn.softmax(s, axis=-1).astype(v.dtype)
    o = jnp.einsum('brnhqk,brnkhd->brnqhd', p, vg)
    return o.reshape(bn, s_len, C_WIDTH)


def hybrid_layer(x, c_act, cos, sin, layer, norm_gain, w_ada, b_ada, w_in, diff_lambda, diff_subln_gain, na_rpb, w_branch, w_out):
    bn, s_len, _ = x.shape
    shift, scale, gate = jnp.split(c_act @ w_ada + b_ada, 3, axis=-1)
    h = rms_norm(x, norm_gain) * (1.0 + scale[:, None]) + shift[:, None]
    proj = h @ w_in
    a_q, a_k, a_v, b_q, b_k, b_v, c_q, c_k, c_v, z, g = split_cols(proj, IN_SPLITS)

    a_q = apply_partial_rope(a_q.reshape(bn, s_len, A_HEADS, 2, HEAD_DIM), cos, sin)
    a_k = apply_partial_rope(a_k.reshape(bn, s_len, A_HEADS, 2, HEAD_DIM), cos, sin)
    a_v = a_v.reshape(bn, s_len, A_HEADS, A_VDIM)
    lambda_init = 0.8 - 0.6 * math.exp(-0.3 * layer)
    lq1, lk1, lq2, lk2 = diff_lambda.astype(jnp.float32)
    lam = jnp.exp(jnp.sum(lq1 * lk1)) - jnp.exp(jnp.sum(lq2 * lk2)) + lambda_init
    y_a = diff_attention(a_q, a_k, a_v, lam, lambda_init, diff_subln_gain)

    b_q = apply_partial_rope(b_q.reshape(bn, s_len, B_GROUPS, B_HEADS, HEAD_DIM), cos, sin)
    b_k = apply_partial_rope(b_k.reshape(bn, s_len, B_GROUPS, B_HEADS, HEAD_DIM), cos, sin)
    b_v = b_v.reshape(bn, s_len, B_GROUPS, B_HEADS, HEAD_DIM)
    y_b = dilated_mixture(b_q, b_k, b_v)

    y_c = neighborhood_attention(c_q.reshape(bn, s_len, C_HEADS, HEAD_DIM), c_k.reshape(bn, s_len, C_HEADS, HEAD_DIM), c_v.reshape(bn, s_len, C_HEADS, HEAD_DIM), na_rpb)

    y = jnp.concatenate([y_a, y_b, y_c], axis=-1) * jax.nn.silu(z)
    gates = jax.nn.sigmoid(g.reshape(bn, s_len, N_BRANCHES, D_MODEL))
    bounds = ((0, A_WIDTH), (A_WIDTH, A_WIDTH + B_WIDTH), (A_WIDTH + B_WIDTH, BR_WIDTH))
    merged = gates[:, :, 0] * (y[..., bounds[0][0]:bounds[0][1]] @ w_branch[bounds[0][0]:bounds[0][1]])
    merged = merged + gates[:, :, 1] * (y[..., bounds[1][0]:bounds[1][1]] @ w_branch[bounds[1][0]:bounds[1][1]])
    merged = merged + gates[:, :, 2] * (y[..., bounds[2][0]:bounds[2][1]] @ w_branch[bounds[2][0]:bounds[2][1]])
    out = merged @ w_out
    return x + gate[:, None] * out


def setup_inputs(seed: int = 0) -> dict:
    key = jax.random.key(seed)
    ks = jax.random.split(key, 13)
    f32 = jnp.float32
    nrm = jax.random.normal
    x = nrm(ks[0], (BATCH, SEQ, D_MODEL), f32)
    c = nrm(ks[1], (BATCH, D_MODEL), f32)
    offs = jax.random.randint(ks[2], (BATCH, 1), 0, 1024, dtype=jnp.int32)
    positions = jnp.arange(SEQ, dtype=jnp.int32)[None, :] + offs
    norm_gain = 1.0 + 0.05 * nrm(ks[3], (DEPTH, D_MODEL), f32)
    w_ada = (0.5 * D_MODEL ** -0.5) * nrm(ks[4], (DEPTH, D_MODEL, 3 * D_MODEL), f32)
    b_ada = 0.01 * nrm(ks[5], (DEPTH, 3 * D_MODEL), f32)
    w_in = (D_MODEL ** -0.5) * nrm(ks[6], (DEPTH, D_MODEL, IN_COLS), f32)
    diff_lambda = 0.1 * nrm(ks[7], (DEPTH, 4, HEAD_DIM), f32)
    diff_subln_gain = 1.0 + 0.05 * nrm(ks[8], (DEPTH, A_VDIM), f32)
    na_rpb = 0.2 * nrm(ks[9], (DEPTH, C_HEADS, 2 * NA_KH - 1, 2 * NA_KW - 1), f32)
    br_scale = jnp.concatenate([jnp.full((A_WIDTH,), A_WIDTH ** -0.5, f32), jnp.full((B_WIDTH,), B_WIDTH ** -0.5, f32), jnp.full((C_WIDTH,), C_WIDTH ** -0.5, f32)])
    w_branch = nrm(ks[10], (DEPTH, BR_WIDTH, D_MODEL), f32) * br_scale[None, :, None]
    w_out = (D_MODEL ** -0.5) * nrm(ks[11], (DEPTH, D_MODEL, D_MODEL), f32)
    final_gain = 1.0 + 0.05 * nrm(ks[12], (D_MODEL,), f32)
    return {'x': x, 'c': c, 'positions': positions, 'norm_gain': norm_gain, 'w_ada': w_ada, 'b_ada': b_ada, 'w_in': w_in, 'diff_lambda': diff_lambda, 'diff_subln_gain': diff_subln_gain, 'na_rpb': na_rpb, 'w_branch': w_branch, 'w_out': w_out, 'final_gain': final_gain}


def reference(x, c, positions, norm_gain, w_ada, b_ada, w_in, diff_lambda, diff_subln_gain, na_rpb, w_branch, w_out, final_gain):
    cos, sin = rope_tables(positions)
    c_act = jax.nn.silu(c)
    for layer in range(DEPTH):
        x = hybrid_layer(x, c_act, cos, sin, layer, norm_gain[layer], w_ada[layer], b_ada[layer], w_in[layer], diff_lambda[layer], diff_subln_gain[layer], na_rpb[layer], w_branch[layer], w_out[layer])
    return rms_norm(x, final_gain)
```

```python
import math
from contextlib import ExitStack

import numpy as np
import concourse.bass as bass
import concourse.mybir as mybir
from concourse.bass_utils import run_bass_kernel_spmd

F32 = mybir.dt.float32
BF16 = mybir.dt.bfloat16
I32 = mybir.dt.int32
AF = mybir.ActivationFunctionType
ALU = mybir.AluOpType

D = 1024
S_LEN = 4096
DEPTH = 2
NCORES = 8
S_OWN = S_LEN // 2
NT = S_OWN // 128
NTK = S_LEN // 128
SBT = 2048
NSB = S_OWN // SBT
NQB = S_OWN // 512
NWT = 32
EPS = 1e-6
NEG = -30000.0
TWO_PI = float(2 * np.pi)
WP_COLS = 8192
VX_COLS = 1556
ENGS = ["pe", "act", "dve", "pool", "sp"]


class Buf:
    def __init__(self, name, dram=False):
        self.name = name
        self.dram = dram
        self.w = {}
        self.r = {}
        self.semkey = None


class Sched:
    def __init__(self, nc, stack):
        self.nc = nc
        self.stack = stack
        self.ops = {e: [] for e in ENGS}
        self.seq = {e: 0 for e in ENGS}
        self.known = {e: {} for e in ENGS}
        self.semh = {}
        for e in ENGS:
            self.semh[("e", e)] = stack.enter_context(nc.semaphore("s_" + e))
        self.dcnt = {}
        self.free = []
        self.nd = 0
        self.live = []

    def buf(self, name, dram=False):
        b = Buf(name, dram)
        if not dram:
            self.live.append(b)
        return b

    def _wait(self, eng, key, val):
        if key == ("e", "pe") and eng == "pe":
            return
        if self.known[eng].get(key, 0) >= val:
            return
        self.known[eng][key] = val
        self.ops[eng].append(("wait", self.semh[key], val))

    def _deps(self, eng, reads, writes):
        for b in reads:
            for k, v in b.w.items():
                self._wait(eng, k, v)
        for b in writes:
            if not b.dram:
                for k, v in b.w.items():
                    self._wait(eng, k, v)
            for k, v in b.r.items():
                self._wait(eng, k, v)

    def _commit(self, key, val, reads, writes):
        for b in reads:
            if b.r.get(key, 0) < val:
                b.r[key] = val
        for b in writes:
            if b.dram:
                if b.w.get(key, 0) < val:
                    b.w[key] = val
            else:
                b.w = {key: val}
                b.r = {}

    def op(self, eng, fn, reads=(), writes=()):
        self._deps(eng, reads, writes)
        self.seq[eng] += 1
        key = ("e", eng)
        self.ops[eng].append(("op", fn, self.semh[key], 1))
        self._commit(key, self.seq[eng], reads, writes)

    def _dsem(self, b):
        if b.semkey is None:
            if self.free:
                b.semkey = self.free.pop()
            else:
                key = ("d", self.nd)
                self.nd += 1
                self.semh[key] = self.stack.enter_context(self.nc.semaphore("d%d" % key[1]))
                self.dcnt[key] = 0
                b.semkey = key
        return b.semkey

    def dma(self, eng, fn, src, dst, sembuf=None):
        if sembuf is None:
            sembuf = src if dst.dram else dst
        key = self._dsem(sembuf)
        self._deps(eng, [src], [dst])
        self.dcnt[key] += 16
        self.ops[eng].append(("op", fn, self.semh[key], 16))
        self._commit(key, self.dcnt[key], [src], [dst])

    def barrier(self):
        for e in ENGS:
            for x in ENGS:
                if x != e and self.seq[x] > 0:
                    self._wait(e, ("e", x), self.seq[x])
            for k, v in self.dcnt.items():
                if v > 0:
                    self._wait(e, k, v)

    def end_phase(self):
        self.barrier()
        for b in self.live:
            if b.semkey is not None:
                self.free.append(b.semkey)
                b.semkey = None
        self.live = []

    def replay(self, name, eng):
        for o in self._emit_ops[name]:
            if o[0] == "wait":
                eng.wait_ge(o[1], o[2])
            else:
                o[1](eng).then_inc(o[2], o[3])

    def emit(self):
        ops, self.ops = self.ops, {e: [] for e in ENGS}
        self._emit_ops = ops
        with self.nc.Block() as block:
            @block.tensor
            def _(e):
                self.replay("pe", e)

            @block.scalar
            def _(e):
                self.replay("act", e)

            @block.vector
            def _(e):
                self.replay("dve", e)

            @block.gpsimd
            def _(e):
                self.replay("pool", e)

            @block.sync
            def _(e):
                self.replay("sp", e)


def lambda_init(layer):
    return 0.8 - 0.6 * math.exp(-0.3 * layer)


B_WIN = (1, 2, 8)
B_MI0 = (0, 3, 8)


def c_deltas(j):
    if j == 0:
        return [-2, -1, 0, 1, 2, 3]
    if j == NT - 1:
        return [-3, -2, -1, 0, 1, 2]
    return [-2, -1, 0, 1, 2]


def c_type(j):
    return {0: 1, 1: 2, NT - 2: 3, NT - 1: 4}.get(j, 0)


def build_program(dbg=False, layers=DEPTH, phases="PABCM"):
    nc = bass.Bass("TRN2", target_bir_lowering=False, num_devices=NCORES)

    def din(name, shape, dt=F32):
        return nc.dram_tensor(name, shape, dt, kind="ExternalInput").ap()

    def dscr(name, shape, dt):
        ext = dbg and name in ("x1_d",)
        return nc.dram_tensor(name, shape, dt, kind="ExternalOutput" if ext else "Internal").ap()

    x_in = din("x", [S_OWN, D])
    c_in = din("c", [1, D])
    pos_in = din("pos", [1, S_OWN], I32)
    ng_in = din("norm_gain", [DEPTH, D])
    wada_in = din("w_ada", [DEPTH, D, 3 * D])
    bada_in = din("b_ada", [DEPTH, 3 * D])
    wp_in = din("wp", [DEPTH, D, WP_COLS])
    wg_in = din("wg", [DEPTH, D, 3 * D])
    dl_in = din("diff_lambda", [DEPTH, 256])
    sg_in = din("subln", [DEPTH, 128])
    rpb_in = din("rpbg", [DEPTH, 128, 7 * 4 * 128])
    wb_in = din("w_branch", [DEPTH, D, D])
    wo_in = din("w_out", [DEPTH, D, D])
    fg_in = din("final_gain", [1, D])
    ident_in = din("ident", [128, 128])
    invc_in = din("inv_col", [128, 1])
    sgnc_in = din("sign_col", [128, 1])
    mbb_in = din("mbb", [128, 25 * 128])
    mc_in = din("mc", [128, 5 * 7 * 128])
    kv_in = din("kv", [128, NWT])
    out_d = nc.dram_tensor("out", [S_OWN, D], F32, kind="ExternalOutput").ap()

    cf_d = dscr("cf_d", [128, S_OWN], F32)
    sf_d = dscr("sf_d", [128, S_OWN], F32)
    hT_d = dscr("hT_d", [128, 8, S_OWN], BF16)
    qkT_d = dscr("qkT_d", [24, 128, S_OWN], BF16)
    vx_d = dscr("vx_d", [S_OWN, VX_COLS], BF16)
    zs_d = dscr("zs_d", [S_OWN, D], F32)
    yz_d = dscr("yz_d", [S_OWN, D], BF16)
    x1_d = dscr("x1_d", [S_OWN, D], F32)
    ksh_d = [nc.dram_tensor("ksh%d" % l, [2, 12 * 128, S_OWN], BF16, kind="Internal", addr_space="Shared").ap()
             for l in range(DEPTH)]
    vsh_d = [nc.dram_tensor("vsh%d" % l, [2, S_OWN, VX_COLS], BF16, kind="Internal", addr_space="Shared").ap()
             for l in range(DEPTH)]
    ada_d = dscr("ada_d", [DEPTH, 3 * D], F32)

    with ExitStack() as top:
        S = Sched(nc, top)
        DIN = S.buf("din", dram=True)
        B_tab = S.buf("tab_d", dram=True)
        B_hT = S.buf("hT_d", dram=True)
        B_qk = S.buf("qkT_d", dram=True)
        B_vx = S.buf("vx_d", dram=True)
        B_zs = S.buf("zs_d", dram=True)
        B_yz = S.buf("yz_d", dram=True)
        B_x1 = S.buf("x1_d", dram=True)
        B_ada = S.buf("ada_d", dram=True)
        B_out = S.buf("out_d", dram=True)
        B_ksh = [S.buf("ksh%d" % l, dram=True) for l in range(DEPTH)]
        B_vsh = [S.buf("vsh%d" % l, dram=True) for l in range(DEPTH)]
        _par = {}

        def PAR(e, other=False):
            k = id(e)
            if k not in _par:
                p = e.partition_id() % 2
                _par[k] = (p, 1 - p)
            return _par[k][1 if other else 0]

        uid = [0]

        def alloc(st, name, shape, dt):
            uid[0] += 1
            nm = "%s_%d" % (name, uid[0])
            return st.enter_context(nc.sbuf_tensor(nm, shape, dt)), S.buf(nm)

        def palloc(st, name, shape, dt):
            uid[0] += 1
            nm = "%s_%d" % (name, uid[0])
            return st.enter_context(nc.psum_tensor(nm, shape, dt)), S.buf(nm)

        def MM(out, lhsT, rhs, start, stop, R, W, skip=False):
            if skip:
                S.op("pe", lambda e: e.matmul(out, lhsT=lhsT, rhs=rhs, start=start, stop=stop,
                                              skip_group_check=True), R, W)
            else:
                S.op("pe", lambda e: e.matmul(out, lhsT=lhsT, rhs=rhs, start=start, stop=stop), R, W)

        def TR(out, in_, ident, R, W):
            S.op("pe", lambda e: e.transpose(out, in_, ident), R, W)

        def ACT(out, in_, func, R, W, scale=None, bias=None, accum=None):
            kw = {}
            if scale is not None:
                kw["scale"] = scale
            if bias is not None:
                kw["bias"] = bias
            if accum is not None:
                kw["accum_out"] = accum
            S.op("act", lambda e: e.activation(out=out, in_=in_, func=func, **kw), R, W)

        def TS(eng, out, in0, s1, s2, op0, op1, R, W):
            if op1 is None:
                S.op(eng, lambda e: e.tensor_scalar(out=out, in0=in0, scalar1=s1, scalar2=None, op0=op0), R, W)
            else:
                S.op(eng, lambda e: e.tensor_scalar(out=out, in0=in0, scalar1=s1, scalar2=s2, op0=op0, op1=op1), R, W)

        def TT(eng, out, in0, in1, op, R, W):
            S.op(eng, lambda e: e.tensor_tensor(out=out, in0=in0, in1=in1, op=op), R, W)

        def STT(eng, out, in0, scalar, in1, op0, op1, R, W):
            S.op(eng, lambda e: e.scalar_tensor_tensor(out=out, in0=in0, scalar=scalar, in1=in1,
                                                       op0=op0, op1=op1), R, W)

        def CP(eng, out, in_, R, W):
            S.op(eng, lambda e: e.tensor_copy(out=out, in_=in_), R, W)

        def RCP(out, in_, R, W):
            S.op("dve", lambda e: e.reciprocal(out=out, in_=in_), R, W)

        def MSET(eng, ap, val, W):
            S.op(eng, lambda e: e.memset(ap, val), [], W)

        def DMA(eng, out, in_, src, dst, sembuf=None, slow=False):
            if slow:
                S.dma(eng, lambda e: e.dma_start(out=out, in_=in_, allow_slow_non_contiguous=True), src, dst, sembuf)
            else:
                S.dma(eng, lambda e: e.dma_start(out=out, in_=in_), src, dst, sembuf)

        def rstd_ops(st_tag, ss, B_ss, rs, B_rs, nhalf, B_nh, n):
            TS("pool", rs, ss, 1.0 / n, EPS, ALU.mult, ALU.add, [B_ss], [B_rs])
            TT("pool", rs, rs, nhalf, ALU.pow, [B_rs, B_nh], [B_rs])

        ident, B_id = alloc(top, "ident", [128, 128], BF16)
        nhalf, B_nh = alloc(top, "nhalf", [128, 1], F32)
        DMA("pool", ident[:], ident_in, DIN, B_id)
        MSET("pool", nhalf[:], -0.5, [B_nh])

        with ExitStack() as ph:
            CH = 2048
            invc, B_invc = alloc(ph, "invc", [128, 1], F32)
            sgnc, B_sgnc = alloc(ph, "sgnc", [128, 1], F32)
            posi, B_posi = alloc(ph, "posi", [128, CH], I32)
            posf, B_posf = alloc(ph, "posf", [128, CH], F32)
            ang, B_ang = alloc(ph, "ang", [128, CH], F32)
            ki, B_ki = alloc(ph, "ki", [128, CH], I32)
            kf, B_kf = alloc(ph, "kf", [128, CH], F32)
            m1, B_m1 = alloc(ph, "m1", [128, CH], F32)
            tab, B_tb = alloc(ph, "tab", [128, CH], F32)
            DMA("sp", invc[:], invc_in, DIN, B_invc)
            DMA("sp", sgnc[:], sgnc_in, DIN, B_sgnc)
            for ch in range(S_OWN // CH):
                DMA("sp", posi[:], pos_in[:, ch * CH:(ch + 1) * CH].partition_broadcast(128), DIN, B_posi)
                CP("dve", posf[:], posi[:], [B_posi], [B_posf])
                for which in range(2):
                    TS("dve", ang[:], posf[:], invc[:, 0:1], (math.pi / 2 if which == 0 else 0.0),
                       ALU.mult, ALU.add, [B_posf, B_invc], [B_ang])
                    TS("dve", ki[:], ang[:], 1.0 / TWO_PI, None, ALU.mult, None, [B_ang], [B_ki])
                    CP("dve", kf[:], ki[:], [B_ki], [B_kf])
                    STT("dve", ang[:], kf[:], -TWO_PI, ang[:], ALU.mult, ALU.add, [B_kf, B_ang], [B_ang])
                    TS("dve", m1[:], ang[:], -math.pi, TWO_PI, ALU.is_lt, ALU.mult, [B_ang], [B_m1])
                    TT("dve", ang[:], ang[:], m1[:], ALU.add, [B_ang, B_m1], [B_ang])
                    TS("dve", m1[:], ang[:], math.pi, -TWO_PI, ALU.is_gt, ALU.mult, [B_ang], [B_m1])
                    TT("dve", ang[:], ang[:], m1[:], ALU.add, [B_ang, B_m1], [B_ang])
                    if which == 0:
                        ACT(tab[:], ang[:], AF.Sin, [B_ang], [B_tb])
                        DMA("sp", cf_d[:, ch * CH:(ch + 1) * CH], tab[:], B_tb, B_tab)
                    else:
                        ACT(tab[:], ang[:], AF.Sin, [B_ang, B_sgnc], [B_tb], scale=sgnc[:, 0:1])
                        DMA("sp", sf_d[:, ch * CH:(ch + 1) * CH], tab[:], B_tb, B_tab)
            S.end_phase()

        def run_pipeline(events, la):
            n = len(events)
            for i in range(n + la):
                if i < n:
                    ev = events[i]
                    for f in ev.get("pre", ()):
                        f()
                    ev["s1"]()
                    ev["s2"]()
                k = i - la
                if k >= 0:
                    ev = events[k]
                    ev["s3"]()
                    for f in ev.get("post", ()):
                        f()

        def phase_P(l):
            x_src, B_xs = (x_in, DIN) if l == 0 else (x1_d, B_x1)
            with ExitStack() as ph:
                hT, B_h = alloc(ph, "hT", [128, 8, SBT], BF16)
                Cf, B_cf = alloc(ph, "Cf", [128, SBT], F32)
                Sf, B_sf = alloc(ph, "Sf", [128, SBT], F32)
                wm = [alloc(ph, "wm%d" % i, [128, 8, 512], BF16) for i in range(2)]
                ws = [alloc(ph, "ws%d" % i, [128, 8, 512], BF16) for i in range(2)]
                wv, B_wv = alloc(ph, "wv", [128, 8, 1536], BF16)
                xt = [alloc(ph, "xt%d" % i, [128, D], F32) for i in range(2)]
                tmpn = [alloc(ph, "tmpn%d" % i, [128, D], F32) for i in range(2)]
                ht = [alloc(ph, "ht%d" % i, [128, D], BF16) for i in range(2)]
                junk, B_junk = alloc(ph, "junk", [128, D], BF16)
                ss = [alloc(ph, "ss%d" % i, [128, 1], F32) for i in range(2)]
                rs = [alloc(ph, "rs%d" % i, [128, 1], F32) for i in range(2)]
                adab, B_adab = alloc(ph, "adab", [128, 3 * D], F32)
                gs, B_gs = alloc(ph, "gs", [128, D], F32)
                cT32, B_cT32 = alloc(ph, "cT32", [128, 8], F32)
                cTf, B_cTf = alloc(ph, "cTf", [128, 8], F32)
                arow = [alloc(ph, "arow%d" % i, [1, 512], F32) for i in range(2)]
                brow = [alloc(ph, "brow%d" % i, [1, 512], F32) for i in range(2)]
                t1 = [alloc(ph, "t1_%d" % i, [128, 512], F32) for i in range(2)]
                t2 = [alloc(ph, "t2_%d" % i, [128, 512], F32) for i in range(2)]
                ob = [alloc(ph, "ob%d" % i, [128, 512], BF16) for i in range(3)]
                vst = [alloc(ph, "vst%d" % i, [128, VX_COLS], BF16) for i in range(2)]
                zst = [alloc(ph, "zst%d" % i, [128, 512], F32) for i in range(2)]
                pm = [palloc(ph, "pm%d" % i, [128, 512], F32) for i in range(2)]
                psw = [palloc(ph, "psw%d" % i, [128, 512], F32) for i in range(2)]
                pv = [palloc(ph, "pv%d" % i, [128, 512], F32) for i in range(2)]
                ptr, B_ptr = palloc(ph, "ptr", [128, D], BF16)
                prow, B_prow = palloc(ph, "prow", [128, 512], F32)

                DMA("sp", cT32[:], c_in.rearrange("o (c p) -> p (o c)", p=128), DIN, B_cT32, slow=True)
                ACT(cTf[:], cT32[:], AF.Silu, [B_cT32], [B_cTf])
                stage = [[(Cf, B_cf, 0), (Cf, B_cf, 1024), (Sf, B_sf, 0)],
                         [(xt[0][0], xt[0][1], 0), (xt[1][0], xt[1][1], 0), (tmpn[0][0], tmpn[0][1], 0)]]
                accs6 = [pm[0], pm[1], psw[0], psw[1], pv[0], pv[1]]
                for c in range(8):
                    stg = stage[c % 2]
                    for pi, (buf_t, B_b, o) in enumerate(stg):
                        DMA("sp", buf_t[:, o:o + 1024], wada_in[l, c * 128:(c + 1) * 128, pi * 1024:(pi + 1) * 1024], DIN, B_b)
                    for nb in range(6):
                        buf_t, B_b, o = stg[nb // 2]
                        a_p, B_ap = accs6[nb]
                        MM(a_p[0:1, :], cTf[:, c:c + 1], buf_t[:, o + (nb % 2) * 512:o + (nb % 2 + 1) * 512],
                           c == 0, c == 7, [B_cTf, B_b], [B_ap])
                for nb in range(6):
                    a_p, B_ap = accs6[nb]
                    a_r, B_ar = arow[nb % 2]
                    b_r, B_br = brow[nb % 2]
                    DMA("sp", b_r[:], bada_in[l:l + 1, nb * 512:(nb + 1) * 512], DIN, B_br)
                    TT("dve", a_r[:], a_p[0:1, :], b_r[:], ALU.add, [B_ap, B_br], [B_ar])
                    DMA("sp", ada_d[l:l + 1, nb * 512:(nb + 1) * 512], a_r[:], B_ar, B_ada)
                DMA("sp", adab[:], ada_d[l:l + 1, :].partition_broadcast(128), B_ada, B_adab)
                DMA("sp", gs[:], ng_in[l:l + 1, :].partition_broadcast(128), DIN, B_gs)
                STT("dve", gs[:], adab[:, D:2 * D], 1.0, gs[:], ALU.add, ALU.mult, [B_adab, B_gs], [B_gs])
                for v_t, B_v in vst:
                    MSET("pool", v_t[:], 1.0, [B_v])

                wpv = wp_in[l].rearrange("(c p) n -> p c n", p=128)
                for sbk in range(NSB):
                    t0 = sbk * SBT
                    DMA("sp", Cf[:], cf_d[:, t0:t0 + SBT], B_tab, B_cf)
                    DMA("sp", Sf[:], sf_d[:, t0:t0 + SBT], B_tab, B_sf)
                    for t in range(SBT // 128):
                        i = t % 2
                        x_t, B_x = xt[i]
                        tm, B_tm = tmpn[i]
                        h_t, B_ht = ht[i]
                        s_t, B_s = ss[i]
                        r_t, B_r = rs[i]
                        tok = t0 + t * 128
                        DMA("sp", x_t[:], x_src[tok:tok + 128, :], B_xs, B_x)
                        MSET("pool", s_t[:], 0.0, [B_s])
                        ACT(junk[:], x_t[:], AF.Square, [B_x], [B_junk, B_s], accum=s_t[:])
                        rstd_ops("n", s_t[:], B_s, r_t[:], B_r, nhalf[:], B_nh, D)
                        STT("dve", tm[:], x_t[:], r_t[:, 0:1], gs[:], ALU.mult, ALU.mult, [B_x, B_r, B_gs], [B_tm])
                        TT("pool", h_t[:], tm[:], adab[:, 0:D], ALU.add, [B_tm, B_adab], [B_ht])
                        for c in range(8):
                            TR(ptr[:, c * 128:(c + 1) * 128], h_t[:, c * 128:(c + 1) * 128], ident[:], [B_ht, B_id], [B_ptr])
                        ACT(hT[:, :, t * 128:(t + 1) * 128], ptr[:].rearrange("p (c t) -> p c t", c=8), AF.Copy,
                            [B_ptr], [B_h])
                    DMA("act", hT_d[:, :, t0:t0 + SBT], hT[:], B_h, B_hT)

                    groups = [(0, 1024, [0, 1, 2, 3]), (512, 1536, [12, 13, 14, 15]),
                              (2048, 3584, [4, 5, 6, 7]), (2560, 4096, [8, 9, 16, 17]), (3072, 4608, [18, 19, 20, 21]),
                              (5120, None, [10, 11, 22, 23])]
                    for gi, (cb, sbase, chl) in enumerate(groups):
                        nch = len(chl)
                        w_t, B_w = wm[gi % 2]
                        DMA("pool", w_t[:], wpv[:, :, cb:cb + 512], DIN, B_w)
                        if sbase is not None:
                            s_w, B_sw = ws[gi % 2]
                            DMA("pool", s_w[:], wpv[:, :, sbase:sbase + 512], DIN, B_sw)
                        for ci in range(nch):
                            for tb in range(SBT // 512):
                                k = (ci * 4 + tb) % 2
                                p_m, B_pm = pm[k]
                                for c in range(8):
                                    MM(p_m[:], w_t[:, c, ci * 128:(ci + 1) * 128], hT[:, c, tb * 512:(tb + 1) * 512],
                                       c == 0, c == 7, [B_w, B_h], [B_pm])
                                o_t, B_o = ob[(ci * 4 + tb) % 3]
                                dst = qkT_d[chl[ci], :, t0 + tb * 512:t0 + (tb + 1) * 512]
                                if sbase is not None:
                                    p_s, B_ps = psw[k]
                                    for c in range(8):
                                        MM(p_s[:], s_w[:, c, ci * 128:(ci + 1) * 128], hT[:, c, tb * 512:(tb + 1) * 512],
                                           c == 0, c == 7, [B_sw, B_h], [B_ps])
                                    a_t, B_a = t1[k]
                                    b_t, B_b = t2[k]
                                    TT("dve", a_t[:], p_m[:], Cf[:, tb * 512:(tb + 1) * 512], ALU.mult, [B_pm, B_cf], [B_a])
                                    TT("dve", b_t[:], p_s[:], Sf[:, tb * 512:(tb + 1) * 512], ALU.mult, [B_ps, B_sf], [B_b])
                                    TT("pool", o_t[:], a_t[:], b_t[:], ALU.add, [B_a, B_b], [B_o])
                                    DMA("pool", dst, o_t[:], B_o, B_qk)
                                else:
                                    ACT(o_t[:], p_m[:], AF.Copy, [B_pm], [B_o])
                                    DMA("act", dst, o_t[:], B_o, B_qk)

                    cpb = S.buf("cpb")
                    for q2 in range(2):
                        S.dma("sp", (lambda q2: (lambda e: e.dma_start(
                            out=ksh_d[l][bass.ds(PAR(e), 1), q2 * 768:(q2 + 1) * 768, :].rearrange("o n t -> (o n) t"),
                            in_=qkT_d[12 + q2 * 6:18 + q2 * 6, :, :].rearrange("c p t -> (c p) t"))))(q2),
                            B_qk, B_ksh[l], cpb)
                    for vb in range(3):
                        DMA("pool", wv[:, :, vb * 512:(vb + 1) * 512], wpv[:, :, 5632 + vb * 512:5632 + (vb + 1) * 512],
                            DIN, B_wv)
                    for t in range(SBT // 128):
                        v_t, B_v = vst[t % 2]
                        tok = t0 + t * 128
                        for vb in range(3):
                            p_v, B_pv = pv[vb % 2]
                            for c in range(8):
                                MM(p_v[:], hT[:, c, t * 128:(t + 1) * 128], wv[:, c, vb * 512:(vb + 1) * 512],
                                   c == 0, c == 7, [B_h, B_wv], [B_pv])
                            if vb == 0:
                                CP("dve", v_t[:, 0:516].rearrange("p (h e) -> p h e", e=129)[:, :, 0:128],
                                   p_v[:].rearrange("p (h e) -> p h e", e=128), [B_pv], [B_v])
                            else:
                                o0 = 516 + (vb - 1) * 520
                                CP("dve", v_t[:, o0:o0 + 520].rearrange("p (h e) -> p h e", e=65)[:, :, 0:64],
                                   p_v[:].rearrange("p (h e) -> p h e", e=64), [B_pv], [B_v])
                        DMA("pool", vx_d[tok:tok + 128, :], v_t[:], B_v, B_vx)
                    for q2 in range(2):
                        S.dma("sp", (lambda q2: (lambda e: e.dma_start(
                            out=vsh_d[l][bass.ds(PAR(e), 1), q2 * 1024:(q2 + 1) * 1024, :].rearrange("o n c -> (o n) c"),
                            in_=vx_d[q2 * 1024:(q2 + 1) * 1024, :])))(q2),
                            B_vx, B_vsh[l], cpb)
                    for zb in range(2):
                        w_t, B_w = wm[zb % 2]
                        DMA("pool", w_t[:], wpv[:, :, 7168 + zb * 512:7168 + (zb + 1) * 512], DIN, B_w)
                        for t in range(SBT // 128):
                            z_t, B_z = zst[t % 2]
                            p_v, B_pv = pv[t % 2]
                            tok = t0 + t * 128
                            for c in range(8):
                                MM(p_v[:], hT[:, c, t * 128:(t + 1) * 128], w_t[:, c, :], c == 0, c == 7, [B_h, B_w], [B_pv])
                            ACT(z_t[:], p_v[:], AF.Silu, [B_pv], [B_z])
                            DMA("act", zs_d[tok:tok + 128, zb * 512:(zb + 1) * 512], z_t[:], B_z, B_zs)
                S.end_phase()
            S.emit()
            nc.all_core_barrier()

        def phase_A(l):
            li = lambda_init(l)
            with ExitStack() as ph:
                dl, B_dl = alloc(ph, "dl", [128, 256], F32)
                pr, B_pr = alloc(ph, "pr", [128, 128], F32)
                s12, B_s12 = alloc(ph, "s12", [128, 2], F32)
                e12, B_e12 = alloc(ph, "e12", [128, 2], F32)
                nlam, B_nlam = alloc(ph, "nlam", [128, 1], F32)
                sgb, B_sgb = alloc(ph, "sgb", [128, 128], F32)
                KT = [alloc(ph, "KT%d" % i, [128, S_LEN], BF16) for i in range(2)]
                VX = [alloc(ph, "VX%d" % i, [128, NTK, 129], BF16) for i in range(2)]
                QT = [alloc(ph, "QT%d" % i, [128, 512], BF16) for i in range(2)]
                ZS = [alloc(ph, "ZS%d" % i, [128, 4, 128], F32) for i in range(2)]
                E = [alloc(ph, "E%d" % i, [128, 512], BF16) for i in range(4)]
                st = [palloc(ph, "st%d" % i, [128, 512], F32) for i in range(4)]
                acc = [palloc(ph, "acc%d" % i, [128, 512], F32) for i in range(3)]
                rd, B_rd = alloc(ph, "rd", [128, 2], F32)
                o0, B_o0 = alloc(ph, "o0", [128, 128], F32)
                o1, B_o1 = alloc(ph, "o1", [128, 128], F32)
                oo, B_oo = alloc(ph, "oo", [128, 128], F32)
                jk, B_jk = alloc(ph, "jk", [128, 128], BF16)
                ssq, B_ssq = alloc(ph, "ssq", [128, 1], F32)
                rsq, B_rsq = alloc(ph, "rsq", [128, 1], F32)
                yy, B_yy = alloc(ph, "yy", [128, 128], F32)
                yzt = [alloc(ph, "yzt%d" % i, [128, 4, 128], BF16) for i in range(2)]

                DMA("sp", dl[:], dl_in[l:l + 1, :].partition_broadcast(128), DIN, B_dl)
                dlv = dl[:].rearrange("p (a b d) -> p a b d", a=2, b=2)
                TT("dve", pr[:].rearrange("p (a d) -> p a d", a=2), dlv[:, :, 0, :], dlv[:, :, 1, :], ALU.mult, [B_dl], [B_pr])
                S.op("dve", lambda e: e.reduce_sum(out=s12[:], in_=pr[:].rearrange("p (a d) -> p a d", a=2),
                                                   axis=mybir.AxisListType.X), [B_pr], [B_s12])
                ACT(e12[:], s12[:], AF.Exp, [B_s12], [B_e12])
                STT("dve", nlam[:], e12[:, 1:2], -li, e12[:, 0:1], ALU.add, ALU.subtract, [B_e12], [B_nlam])
                DMA("sp", sgb[:], sg_in[l:l + 1, :].partition_broadcast(128), DIN, B_sgb)
                TS("dve", sgb[:], sgb[:], 1.0 - li, None, ALU.mult, None, [B_sgb], [B_sgb])

                def accreg(comp, qi):
                    r = comp * 4 + qi
                    return acc[r // 3][0], acc[r // 3][1], (r % 3) * 130, (r % 3 == 0)

                accs = [alloc(ph, "accs%d" % i, [128, 3, 390], F32) for i in range(2)]
                events = []

                def mk_head_pre(hd):
                    def f():
                        K_t, B_K = KT[hd % 2]
                        V_t, B_V = VX[hd % 2]
                        for r in range(2):
                            DMA("sp", K_t[:, r * S_OWN:(r + 1) * S_OWN],
                                ksh_d[l][r, hd * 128:(hd + 1) * 128, :], B_ksh[l], B_K)
                            vsrc = vsh_d[l][r, :, hd * 129:(hd + 1) * 129].rearrange("(kt p) e -> p kt e", p=128)
                            for q2 in range(2):
                                DMA("sp", V_t[:, r * NT + q2 * 8:r * NT + (q2 + 1) * 8, :], vsrc[:, q2 * 8:(q2 + 1) * 8, :],
                                    B_vsh[l], B_V)
                    return f

                def mk_qb_pre(hd, qb):
                    def f():
                        j = (hd * NQB + qb) % 2
                        q0 = qb * 512
                        DMA("sp", QT[j][0][:], qkT_d[hd, :, q0:q0 + 512], B_qk, QT[j][1])
                        DMA("sp", ZS[j][0][:],
                            zs_d[q0:q0 + 512, hd * 128:(hd + 1) * 128].rearrange("(qi p) e -> p qi e", p=128),
                            B_zs, ZS[j][1])
                    return f

                def mk_event(idx, hd, qb, kc):
                    K_t, B_K = KT[hd % 2]
                    V_t, B_V = VX[hd % 2]
                    Q_t, B_Q = QT[(hd * NQB + qb) % 2]
                    banks = [(2 * idx + c) % 4 for c in range(2)]

                    def s1():
                        for comp in range(2):
                            s_t, B_st = st[banks[comp]]
                            MM(s_t[:], K_t[comp * 64:(comp + 1) * 64, kc * 128:(kc + 1) * 128],
                               Q_t[comp * 64:(comp + 1) * 64, :], True, True, [B_K, B_Q], [B_st])

                    def s2():
                        for comp in range(2):
                            s_t, B_st = st[banks[comp]]
                            e_t, B_e = E[banks[comp]]
                            ACT(e_t[:], s_t[:], AF.Exp, [B_st], [B_e], scale=0.125)

                    def s3():
                        for comp in range(2):
                            e_t, B_e = E[banks[comp]]
                            for qi in range(4):
                                a_t, B_a, off, first = accreg(comp, qi)
                                MM(a_t[:, off:off + 129], e_t[:, qi * 128:(qi + 1) * 128], V_t[:, kc, :],
                                   (kc == 0 and first), kc == NTK - 1, [B_e, B_V], [B_a], skip=True)
                    return {"s1": s1, "s2": s2, "s3": s3}

                def mk_finalize(hd, qb):
                    def f():
                        j = (hd * NQB + qb) % 2
                        Z_t, B_Z = ZS[j]
                        y_t, B_y = yzt[j]
                        sa, B_sa = accs[j]
                        q0 = qb * 512
                        for bk in range(3):
                            CP("dve", sa[:, bk, 0:389], acc[bk][0][:, 0:389], [acc[bk][1]], [B_sa])
                        for qi in range(4):
                            r0_, r1_ = qi, 4 + qi
                            a0 = sa[:, r0_ // 3, (r0_ % 3) * 130:(r0_ % 3) * 130 + 129]
                            a1 = sa[:, r1_ // 3, (r1_ % 3) * 130:(r1_ % 3) * 130 + 129]
                            RCP(rd[:, 0:1], a0[:, 128:129], [B_sa], [B_rd])
                            RCP(rd[:, 1:2], a1[:, 128:129], [B_sa, B_rd], [B_rd])
                            TS("dve", o0[:], a0[:, 0:128], rd[:, 0:1], None, ALU.mult, None, [B_sa, B_rd], [B_o0])
                            TS("dve", o1[:], a1[:, 0:128], rd[:, 1:2], None, ALU.mult, None, [B_sa, B_rd], [B_o1])
                            STT("dve", oo[:], o1[:], nlam[:, 0:1], o0[:], ALU.mult, ALU.add, [B_o1, B_o0, B_nlam], [B_oo])
                            MSET("pool", ssq[:], 0.0, [B_ssq])
                            ACT(jk[:], oo[:], AF.Square, [B_oo], [B_jk, B_ssq], accum=ssq[:])
                            rstd_ops("a", ssq[:], B_ssq, rsq[:], B_rsq, nhalf[:], B_nh, 128)
                            STT("dve", yy[:], oo[:], rsq[:, 0:1], sgb[:], ALU.mult, ALU.mult, [B_oo, B_rsq, B_sgb], [B_yy])
                            TT("pool", y_t[:, qi, :], yy[:], Z_t[:, qi, :], ALU.mult, [B_yy, B_Z], [B_y])
                        DMA("pool", yz_d[q0:q0 + 512, hd * 128:(hd + 1) * 128].rearrange("(qi p) e -> p qi e", p=128),
                            y_t[:], B_y, B_yz)
                    return f

                idx = 0
                for hd in range(4):
                    for qb in range(NQB):
                        for kc in range(NTK):
                            ev = mk_event(idx, hd, qb, kc)
                            pre = []
                            if qb == 0 and kc == 0:
                                pre.append(mk_head_pre(hd))
                            if kc == 0:
                                pre.append(mk_qb_pre(hd, qb))
                            ev["pre"] = pre
                            if kc == NTK - 1:
                                ev["post"] = [mk_finalize(hd, qb)]
                            events.append(ev)
                            idx += 1
                run_pipeline(events, 1)
                S.end_phase()

        def phase_BC(l, which):
            with ExitStack() as ph:
                KT, B_K = alloc(ph, "KTb", [128, 2, S_OWN], BF16)
                QT, B_Q = alloc(ph, "QTb", [128, 2, S_OWN], BF16)
                VX, B_V = alloc(ph, "VXb", [128, NT, 260], BF16)
                nhc = 6 if which == "B" else 2
                hcols = 780 if which == "B" else 260
                hki0 = 4 if which == "B" else 10
                hvc0 = 516 if which == "B" else 516 + 780
                H = S_OWN // 2
                KH, B_KH = alloc(ph, "KH", [128, nhc, 2, H], BF16)
                VH, B_VH = alloc(ph, "VH", [128, 2, 8, hcols], BF16)
                kvb, B_kv = alloc(ph, "kvb", [128, NWT], F32)
                DMA("sp", kvb[:], kv_in, DIN, B_kv)
                hq = "sp"
                for side in range(2):
                    t_lo = H if side == 0 else 0
                    DMA("sp", KH[:, :, side, :],
                        ksh_d[l][side, hki0 * 128:(hki0 + nhc) * 128, t_lo:t_lo + H].rearrange("(c p) t -> p c t", p=128),
                        B_ksh[l], B_KH)
                    for q2 in range(2):
                        DMA("sp", VH[:, side, q2 * 4:(q2 + 1) * 4, :],
                            vsh_d[l][side, t_lo + q2 * 512:t_lo + (q2 + 1) * 512, hvc0:hvc0 + hcols]
                            .rearrange("(kt p) e -> p kt e", p=128), B_vsh[l], B_VH)
                if which == "B":
                    mb, B_mb = alloc(ph, "mb", [128, 25, 128], F32)
                    num, B_num = alloc(ph, "num", [128, NT, 260], F32)
                    DMA("sp", mb[:], mbb_in.rearrange("p (m q) -> p m q", q=128), DIN, B_mb)
                else:
                    Gc, B_G = alloc(ph, "Gc", [128, 7, 4, 128], F32)
                    Mc, B_M = alloc(ph, "Mc", [128, 5, 7, 128], F32)
                    DMA("sp", Gc[:], rpb_in[l].rearrange("p (a h q) -> p a h q", a=7, h=4), DIN, B_G)
                    DMA("sp", Mc[:], mc_in.rearrange("p (t a q) -> p t a q", t=5, a=7), DIN, B_M)
                sbz = [alloc(ph, "sbz%d" % i, [128, 4, 128], F32) for i in range(2)]
                E = [alloc(ph, "Eb%d" % i, [128, 512], BF16) for i in range(3)]
                ZS = [alloc(ph, "ZSb%d" % i, [128, 256], F32) for i in range(2)]
                rd = [alloc(ph, "rdb%d" % i, [128, 4, 1], F32) for i in range(2)]
                yb = [alloc(ph, "yb%d" % i, [128, 4, 64], F32) for i in range(2)]
                yzt = [alloc(ph, "yztb%d" % i, [128, 256], BF16) for i in range(2)]
                st = [palloc(ph, "stb%d" % i, [128, 512], F32) for i in range(3)]
                sto = [palloc(ph, "sto%d" % i, [128, 512], F32) for i in range(3)]
                acc = [palloc(ph, "accb%d" % i, [128, 512], F32) for i in range(2)]
                col0 = 512 if which == "B" else 768

                def finalize(j, src, B_src):
                    i = j % 2
                    z_t, B_z = ZS[i]
                    r_t, B_r = rd[i]
                    y_t, B_y = yb[i]
                    o_t, B_o = yzt[i]
                    tok = j * 128
                    DMA("sp", z_t[:], zs_d[tok:tok + 128, col0:col0 + 256], B_zs, B_z)
                    sv = src.rearrange("p (h e) -> p h e", e=65)
                    RCP(r_t[:], sv[:, :, 64:65], [B_src], [B_r])
                    TT("dve", y_t[:], sv[:, :, 0:64], r_t[:].broadcast_to([128, 4, 64]), ALU.mult, [B_src, B_r], [B_y])
                    TT("pool", o_t[:], y_t[:].rearrange("p h e -> p (h e)"), z_t[:], ALU.mult, [B_y, B_z], [B_o])
                    DMA("pool", yz_d[tok:tok + 128, col0:col0 + 256], o_t[:], B_o, B_yz)

                ngroups = 3 if which == "B" else 1
                events = []

                def mk_group_pre(g):
                    def f():
                        if which == "B":
                            kch, qch, vcol = 16 + 2 * g, 4 + 2 * g, 516 + g * 260
                        else:
                            kch, qch, vcol = 22, 10, 516 + 780
                        for c2 in range(2):
                            DMA("sp", QT[:, c2, :], qkT_d[qch + c2, :, :], B_qk, B_Q)
                            DMA("sp", KT[:, c2, :], qkT_d[kch + c2, :, :], B_qk, B_K)
                        vsrc = vx_d[:, vcol:vcol + 260].rearrange("(kt p) e -> p kt e", p=128)
                        for q2 in range(2):
                            DMA("sp", VX[:, q2 * 8:(q2 + 1) * 8, :], vsrc[:, q2 * 8:(q2 + 1) * 8, :], B_vx, B_V)
                    return f

                def mk_event(idx, g, j, di, dlt, ndl):
                    kt = j + dlt + 8
                    s_t, B_st = st[idx % 3]
                    s_o, B_so = sto[idx % 3]
                    z_t, B_zb = sbz[idx % 2]
                    e_t, B_e = E[idx % 3]
                    a_t, B_a = acc[j % 2]

                    def s1():
                        for h in range(4):
                            r0 = (h % 2) * 64
                            dstb, B_dst = (s_t, B_st) if h % 2 == 0 else (s_o, B_so)
                            hh = h // 2
                            c2 = h // 2
                            if kt < 8:
                                kop, B_kop = KH[r0:r0 + 64, 2 * g + c2, 0, kt * 128:(kt + 1) * 128], B_KH
                            elif kt < 8 + NT:
                                kop, B_kop = KT[r0:r0 + 64, c2, (kt - 8) * 128:(kt - 7) * 128], B_K
                            else:
                                kop, B_kop = KH[r0:r0 + 64, 2 * g + c2, 1, (kt - 8 - NT) * 128:(kt - 7 - NT) * 128], B_KH
                            MM(dstb[:, hh * 128:(hh + 1) * 128], kop,
                               QT[r0:r0 + 64, c2, j * 128:(j + 1) * 128], True, True, [B_kop, B_Q], [B_dst], skip=True)

                    def s2():
                        for par, (srcb, B_srcb) in enumerate(((s_t, B_st), (s_o, B_so))):
                            sv = srcb[:, 0:256].rearrange("p (h q) -> p h q", h=2)
                            zv = z_t[:].rearrange("p (a b) q -> p a b q", b=2)[:, :, par, :]
                            if which == "B":
                                mi = B_MI0[g] + dlt + B_WIN[g]
                                STT("dve", zv, sv, 0.125, mb[:, mi, :].unsqueeze(1).broadcast_to([128, 2, 128]),
                                    ALU.mult, ALU.add, [B_srcb, B_mb], [B_zb])
                            else:
                                gv = Gc[:, dlt + 3, :, :].rearrange("p (a b) q -> p a b q", b=2)[:, :, par, :]
                                STT("dve", zv, sv, 0.125, gv, ALU.mult, ALU.add, [B_srcb, B_G], [B_zb])
                        if which == "C":
                            TT("pool", z_t[:], z_t[:], Mc[:, c_type(j), dlt + 3, :].unsqueeze(1).broadcast_to([128, 4, 128]),
                               ALU.add, [B_zb, B_M], [B_zb])
                        ACT(e_t[:], z_t[:].rearrange("p h q -> p (h q)"), AF.Exp, [B_zb, B_kv], [B_e],
                            bias=kvb[:, kt:kt + 1])

                    def s3():
                        for h in range(4):
                            if kt < 8:
                                vop, B_vop = VH[:, 0, kt, g * 260 + h * 65:g * 260 + (h + 1) * 65], B_VH
                            elif kt < 8 + NT:
                                vop, B_vop = VX[:, kt - 8, h * 65:(h + 1) * 65], B_V
                            else:
                                vop, B_vop = VH[:, 1, kt - 8 - NT, g * 260 + h * 65:g * 260 + (h + 1) * 65], B_VH
                            MM(a_t[:, h * 65:(h + 1) * 65], e_t[:, h * 128:(h + 1) * 128], vop,
                               (di == 0 and h == 0), di == ndl - 1, [B_e, B_vop], [B_a], skip=True)
                    return {"s1": s1, "s2": s2, "s3": s3}

                def mk_post(g, j):
                    def f():
                        a_t, B_a = acc[j % 2]
                        if which == "B":
                            if g == 0:
                                CP("dve", num[:, j, :], a_t[:, 0:260], [B_a], [B_num])
                            else:
                                TT("dve", num[:, j, :], num[:, j, :], a_t[:, 0:260], ALU.add, [B_a, B_num], [B_num])
                        else:
                            finalize(j, a_t[:, 0:260], B_a)
                    return f

                idx = 0
                for g in range(ngroups):
                    events = []
                    for j in range(NT):
                        if which == "B":
                            dl_list = list(range(-B_WIN[g], B_WIN[g] + 1))
                        else:
                            dl_list = c_deltas(j)
                        for di, dlt in enumerate(dl_list):
                            ev = mk_event(idx, g, j, di, dlt, len(dl_list))
                            if j == 0 and di == 0:
                                ev["pre"] = [mk_group_pre(g)]
                            if di == len(dl_list) - 1:
                                ev["post"] = [mk_post(g, j)]
                            events.append(ev)
                            idx += 1
                    run_pipeline(events, 2)
                if which == "B":
                    for j in range(NT):
                        finalize(j, num[:, j, :], B_num)
                S.end_phase()

        def load_M_weights(st, l):
            Wg, B_Wg = alloc(st, "Wg", [128, 8, 3 * D], BF16)
            Wb, B_Wb = alloc(st, "Wb", [128, 8, D], BF16)
            Wo, B_Wo = alloc(st, "Wo", [128, 8, D], BF16)
            wgv = wg_in[l].rearrange("(c p) n -> p c n", p=128)
            for nb in range(6):
                DMA("pool", Wg[:, :, nb * 512:(nb + 1) * 512], wgv[:, :, nb * 512:(nb + 1) * 512], DIN, B_Wg)
            wbv = wb_in[l].rearrange("(c p) n -> p c n", p=128)
            wov = wo_in[l].rearrange("(c p) n -> p c n", p=128)
            for nb in range(2):
                DMA("pool", Wb[:, :, nb * 512:(nb + 1) * 512], wbv[:, :, nb * 512:(nb + 1) * 512], DIN, B_Wb)
                DMA("pool", Wo[:, :, nb * 512:(nb + 1) * 512], wov[:, :, nb * 512:(nb + 1) * 512], DIN, B_Wo)
            return (Wg, B_Wg, Wb, B_Wb, Wo, B_Wo)

        def phase_M(l, mw):
            last = (l == DEPTH - 1)
            Wg, B_Wg, Wb, B_Wb, Wo, B_Wo = mw
            x_src, B_xs = (x_in, DIN) if l == 0 else (x1_d, B_x1)
            with ExitStack() as ph:
                gate, B_gate = alloc(ph, "gate", [128, D], F32)
                fgb, B_fgb = alloc(ph, "fgb", [128, D], F32)
                hTt = [alloc(ph, "hTt%d" % i, [128, 8, 512], BF16) for i in range(2)]
                yzl = [alloc(ph, "yzl%d" % i, [128, 4, D], BF16) for i in range(2)]
                yzT, B_yzT = alloc(ph, "yzT", [128, 8, 512], BF16)
                mT, B_mT = alloc(ph, "mT", [128, 8, 512], BF16)
                sg = [alloc(ph, "sg%d" % i, [128, 512], F32) for i in range(2)]
                tq = [alloc(ph, "tq%d" % i, [128, 512], F32) for i in range(3)]
                uq, B_uq = alloc(ph, "uq", [128, 512], F32)
                xt = [alloc(ph, "xtm%d" % i, [128, D], F32) for i in range(2)]
                xn = [alloc(ph, "xn%d" % i, [128, D], F32) for i in range(2)]
                to, B_to = alloc(ph, "to", [128, 512], F32)
                ssf, B_ssf = alloc(ph, "ssf", [128, 1], F32)
                rsf, B_rsf = alloc(ph, "rsf", [128, 1], F32)
                jkf, B_jkf = alloc(ph, "jkf", [128, D], BF16)
                ptr, B_ptr = palloc(ph, "ptrm", [128, D], BF16)
                pb = [palloc(ph, "pb%d" % i, [128, 512], F32) for i in range(2)]
                pg = [palloc(ph, "pg%d" % i, [128, 512], F32) for i in range(2)]
                po = [palloc(ph, "po%d" % i, [128, 512], F32) for i in range(2)]

                DMA("sp", gate[:], ada_d[l:l + 1, 2 * D:3 * D].partition_broadcast(128), B_ada, B_gate)
                if last:
                    DMA("sp", fgb[:], fg_in.partition_broadcast(128), DIN, B_fgb)

                branches = [(0, 4), (4, 2), (6, 2)]
                it = 0
                for tb in range(S_OWN // 512):
                    h_t, B_h = hTt[tb % 2]
                    y_l, B_yl = yzl[tb % 2]
                    q0 = tb * 512
                    DMA("sp", h_t[:], hT_d[:, :, q0:q0 + 512], B_hT, B_h)
                    DMA("sp", y_l[:], yz_d[q0:q0 + 512, :].rearrange("(tt p) n -> p tt n", p=128), B_yz, B_yl)
                    for jc in range(8):
                        for tt in range(4):
                            TR(ptr[:, tt * 128:(tt + 1) * 128], y_l[:, tt, jc * 128:(jc + 1) * 128], ident[:],
                               [B_yl, B_id], [B_ptr])
                        CP("dve", yzT[:, jc, :], ptr[:, 0:512], [B_ptr], [B_yzT])
                    for ncx in range(8):
                        for bi, (jc0, njc) in enumerate(branches):
                            p_b, B_pb = pb[it % 2]
                            p_g, B_pg = pg[it % 2]
                            s_g, B_sg = sg[it % 2]
                            it += 1
                            for k in range(njc):
                                MM(p_b[:], Wb[:, jc0 + k, ncx * 128:(ncx + 1) * 128], yzT[:, jc0 + k, :],
                                   k == 0, k == njc - 1, [B_Wb, B_yzT], [B_pb])
                            for c in range(8):
                                MM(p_g[:], Wg[:, c, bi * D + ncx * 128:bi * D + (ncx + 1) * 128], h_t[:, c, :],
                                   c == 0, c == 7, [B_Wg, B_h], [B_pg])
                            ACT(s_g[:], p_g[:], AF.Sigmoid, [B_pg], [B_sg])
                            t_q, B_tq = tq[bi]
                            TT("dve", t_q[:], p_b[:], s_g[:], ALU.mult, [B_pb, B_sg], [B_tq])
                        TT("pool", uq[:], tq[0][0][:], tq[1][0][:], ALU.add, [tq[0][1], tq[1][1]], [B_uq])
                        TT("pool", mT[:, ncx, :], uq[:], tq[2][0][:], ALU.add, [B_uq, tq[2][1]], [B_mT])
                    for tt in range(4):
                        i = (tb * 4 + tt) % 2
                        x_t, B_x = xt[i]
                        x_n, B_xn = xn[i]
                        tok = q0 + tt * 128
                        DMA("sp", x_t[:], x_src[tok:tok + 128, :], B_xs, B_x)
                        for nb in range(2):
                            p_o, B_po = po[nb]
                            for jc in range(8):
                                MM(p_o[:], mT[:, jc, tt * 128:(tt + 1) * 128], Wo[:, jc, nb * 512:(nb + 1) * 512],
                                   jc == 0, jc == 7, [B_mT, B_Wo], [B_po])
                            TT("dve", to[:], p_o[:], gate[:, nb * 512:(nb + 1) * 512], ALU.mult, [B_po, B_gate], [B_to])
                            TT("pool", x_n[:, nb * 512:(nb + 1) * 512], to[:], x_t[:, nb * 512:(nb + 1) * 512], ALU.add,
                               [B_to, B_x], [B_xn])
                        if not last:
                            DMA("pool", x1_d[tok:tok + 128, :], x_n[:], B_xn, B_x1)
                        else:
                            o_f, B_of = x_t, B_x
                            MSET("pool", ssf[:], 0.0, [B_ssf])
                            ACT(jkf[:], x_n[:], AF.Square, [B_xn], [B_jkf, B_ssf], accum=ssf[:])
                            rstd_ops("f", ssf[:], B_ssf, rsf[:], B_rsf, nhalf[:], B_nh, D)
                            STT("dve", o_f[:], x_n[:], rsf[:, 0:1], fgb[:], ALU.mult, ALU.mult, [B_xn, B_rsf, B_fgb], [B_of])
                            DMA("pool", out_d[tok:tok + 128, :], o_f[:], B_of, B_out)
                S.end_phase()

        for l in range(layers):
            if "P" in phases:
                phase_P(l)
            if "A" in phases:
                phase_A(l)
            if "B" in phases:
                phase_BC(l, "B")
            with ExitStack() as wst:
                mw = load_M_weights(wst, l)
                if "C" in phases:
                    phase_BC(l, "C")
                if "M" in phases:
                    phase_M(l, mw)
        S.barrier()
        S.emit()
        nc._sched_stats = {e: len(S.ops[e]) for e in ENGS}
    return nc


def _swap_cols(w):
    out = np.zeros_like(w)
    n = w.shape[-1]
    for base in range(0, n, 64):
        out[..., base:base + 8] = w[..., base + 8:base + 16]
        out[..., base + 8:base + 16] = w[..., base:base + 8]
    return out


def _const_tables():
    ident = np.eye(128, dtype=np.float32)
    inv = (np.float32(500000.0) ** (-np.arange(0, 16, 2, dtype=np.float32) / np.float32(16))).astype(np.float32)
    inv_col = np.zeros((128, 1), np.float32)
    sign_col = np.zeros((128, 1), np.float32)
    for r in range(128):
        i = r % 64
        if i < 16:
            inv_col[r, 0] = inv[i % 8]
            sign_col[r, 0] = -1.0 if i < 8 else 1.0
    kk = np.arange(128)[:, None]
    qq = np.arange(128)[None, :]
    mbb = np.zeros((128, 25, 128), np.float32)
    mi = 0
    for g, r in enumerate((1, 4, 16)):
        for dlt in range(-B_WIN[g], B_WIN[g] + 1):
            d = 128 * dlt + kk - qq
            valid = (d % r == 0) & (np.abs(d) <= 64 * r)
            mbb[:, mi, :] = np.where(valid, 0.0, NEG)
            mi += 1
    return ident, inv_col, sign_col, mbb.reshape(128, -1)


def _core_tables(hf):
    kk = np.arange(128)[:, None]
    qq = np.arange(128)[None, :]
    mc = np.zeros((128, 5, 7, 128), np.float32)
    for t, j in enumerate((10, NT * hf, NT * hf + 1, NT * hf + NT - 2, NT * hf + NT - 1)):
        for a, dlt in enumerate(range(-3, 4)):
            qrow = 2 * j + qq // 64
            qcol = qq % 64
            krow = 2 * (j + dlt) + kk // 64
            kcol = kk % 64
            rs = np.clip(qrow - 4, 0, 56)
            cs = np.clip(qcol - 8, 0, 48)
            valid = (krow >= rs) & (krow < rs + 8) & (kcol >= cs) & (kcol < cs + 16) & (krow >= 0) & (krow < 64)
            mc[:, t, a, :] = np.where(valid, 0.0, NEG)
    kv = np.zeros((128, NWT), np.float32)
    for w in range(NWT):
        gt = NT * hf - 8 + w
        if not (0 <= gt < NTK):
            kv[:, w] = NEG
    return mc.reshape(128, -1), kv


def _rpb_gather(na_rpb):
    kk = np.arange(128)[:, None]
    qq = np.arange(128)[None, :]
    L = na_rpb.shape[0]
    out = np.zeros((L, 128, 7, 4, 128), np.float32)
    for a, dlt in enumerate(range(-3, 4)):
        dr = np.clip(2 * dlt + kk // 64 - qq // 64 + 7, 0, 14)
        dc = np.clip(kk % 64 - qq % 64 + 15, 0, 30)
        for h in range(4):
            out[:, :, a, h, :] = na_rpb[:, h][:, dr, dc]
    return out.reshape(L, 128, -1)


def _prep_inputs(x, c, positions, norm_gain, w_ada, b_ada, w_in, diff_lambda, diff_subln_gain, na_rpb,
                 w_branch, w_out, final_gain):
    f = lambda a: np.ascontiguousarray(np.asarray(a, dtype=np.float32))
    w_in = f(w_in)
    qa = w_in[:, :, 0:1024]
    qb = w_in[:, :, 1536:3072]
    qc = np.concatenate([w_in[:, :, 3840:4096], w_in[:, :, 4096:4352]], axis=-1)
    v = np.concatenate([w_in[:, :, 1024:1536], w_in[:, :, 3072:3840], w_in[:, :, 4352:4608]], axis=-1)
    z = w_in[:, :, 4608:5632]
    wp = np.ascontiguousarray(np.concatenate([qa, _swap_cols(qa), qb, _swap_cols(qb), qc, v, z], axis=-1))
    assert wp.shape[-1] == WP_COLS
    wg = np.ascontiguousarray(w_in[:, :, 5632:8704])
    ident, inv_col, sign_col, mbb = _const_tables()
    shared = {
        "norm_gain": f(norm_gain), "w_ada": f(w_ada), "b_ada": f(b_ada), "wp": wp, "wg": wg,
        "diff_lambda": f(diff_lambda).reshape(DEPTH, 256), "subln": f(diff_subln_gain),
        "rpbg": _rpb_gather(f(na_rpb)), "w_branch": f(w_branch), "w_out": f(w_out),
        "final_gain": f(final_gain).reshape(1, D), "ident": ident, "inv_col": inv_col, "sign_col": sign_col,
        "mbb": mbb,
    }
    x = f(x)
    c = f(c)
    positions = np.ascontiguousarray(np.asarray(positions, dtype=np.int32))
    maps = []
    core_tabs = [_core_tables(hf) for hf in range(2)]
    for b in range(x.shape[0]):
        for hf in range(2):
            m = dict(shared)
            sl = slice(hf * S_OWN, (hf + 1) * S_OWN)
            m["x"] = np.ascontiguousarray(x[b, sl])
            m["c"] = np.ascontiguousarray(c[b:b + 1])
            m["pos"] = np.ascontiguousarray(positions[b:b + 1, sl])
            m["mc"], m["kv"] = core_tabs[hf]
            maps.append(m)
    return maps


_NC_CACHE = {}


def kernel(x, c, positions, norm_gain, w_ada, b_ada, w_in, diff_lambda, diff_subln_gain, na_rpb,
           w_branch, w_out, final_gain):
    maps = _prep_inputs(x, c, positions, norm_gain, w_ada, b_ada, w_in, diff_lambda, diff_subln_gain, na_rpb,
                        w_branch, w_out, final_gain)
    if "nc" not in _NC_CACHE:
        _NC_CACHE["nc"] = build_program()
    nc = _NC_CACHE["nc"]
    res = run_bass_kernel_spmd(nc, maps, core_ids=list(range(len(maps))))
    outs = [np.asarray(r["out"], dtype=np.float32) for r in res.results]
    nb = len(outs) // 2
    return np.stack([np.concatenate([outs[2 * b], outs[2 * b + 1]], axis=0) for b in range(nb)], axis=0)
```

```python
import math
from contextlib import ExitStack

import numpy as np
import concourse.bass as bass
import concourse.mybir as mybir
from concourse.bass_utils import run_bass_kernel_spmd

F32 = mybir.dt.float32
BF16 = mybir.dt.bfloat16
I32 = mybir.dt.int32
AF = mybir.ActivationFunctionType
ALU = mybir.AluOpType

D = 1024
S_LEN = 4096
DEPTH = 2
NCORES = 8
S_OWN = S_LEN // 2
NT = S_OWN // 128
NTK = S_LEN // 128
SBT = 2048
NSB = S_OWN // SBT
NQB = S_OWN // 512
NWT = 32
EPS = 1e-6
NEG = -30000.0
TWO_PI = float(2 * np.pi)
WP_COLS = 8192
VX_COLS = 1556
ENGS = ["pe", "act", "dve", "pool", "sp"]


class Buf:
    def __init__(self, name, dram=False):
        self.name = name
        self.dram = dram
        self.w = {}
        self.r = {}
        self.semkey = {}


class Sched:
    def __init__(self, nc, stack):
        self.nc = nc
        self.stack = stack
        self.ops = {e: [] for e in ENGS}
        self.seq = {e: 0 for e in ENGS}
        self.known = {e: {} for e in ENGS}
        self.semh = {}
        for e in ENGS:
            self.semh[("e", e)] = stack.enter_context(nc.semaphore("s_" + e))
        self.dcnt = {}
        self.free = {"sw": [], "hw": []}
        self.nd = 0
        self.live = []

    def buf(self, name, dram=False):
        b = Buf(name, dram)
        if not dram:
            self.live.append(b)
        return b

    def _wait(self, eng, key, val):
        if key == ("e", "pe") and eng == "pe":
            return
        if self.known[eng].get(key, 0) >= val:
            return
        self.known[eng][key] = val
        self.ops[eng].append(("wait", self.semh[key], val))

    def _deps(self, eng, reads, writes):
        for b in reads:
            for k, v in b.w.items():
                self._wait(eng, k, v)
        for b in writes:
            if not b.dram:
                for k, v in b.w.items():
                    self._wait(eng, k, v)
            for k, v in b.r.items():
                self._wait(eng, k, v)

    def _commit(self, key, val, reads, writes):
        for b in reads:
            if b.r.get(key, 0) < val:
                b.r[key] = val
        for b in writes:
            if b.dram:
                if b.w.get(key, 0) < val:
                    b.w[key] = val
            else:
                b.w = {key: val}
                b.r = {}

    def op(self, eng, fn, reads=(), writes=()):
        self._deps(eng, reads, writes)
        self.seq[eng] += 1
        key = ("e", eng)
        self.ops[eng].append(("op", fn, self.semh[key], 1))
        self._commit(key, self.seq[eng], reads, writes)

    def _dsem(self, b, cls):
        if cls not in b.semkey:
            if self.free[cls]:
                b.semkey[cls] = self.free[cls].pop()
            else:
                key = ("d", self.nd)
                self.nd += 1
                self.semh[key] = self.stack.enter_context(self.nc.semaphore("d%d" % key[1]))
                self.dcnt[key] = 0
                b.semkey[cls] = key
        return b.semkey[cls]

    def dma(self, eng, fn, src, dst, sembuf=None):
        if sembuf is None:
            sembuf = src if dst.dram else dst
        key = self._dsem(sembuf, "sw" if eng == "pool" else "hw")
        self._deps(eng, [src], [dst])
        self.dcnt[key] += 16
        self.ops[eng].append(("op", fn, self.semh[key], 16))
        self._commit(key, self.dcnt[key], [src], [dst])

    def barrier(self):
        for e in ENGS:
            for x in ENGS:
                if x != e and self.seq[x] > 0:
                    self._wait(e, ("e", x), self.seq[x])
            for k, v in self.dcnt.items():
                if v > 0:
                    self._wait(e, k, v)

    def end_phase(self):
        self.barrier()
        for b in self.live:
            for cls, key in b.semkey.items():
                self.free[cls].append(key)
            b.semkey = {}
        self.live = []

    def replay(self, name, eng):
        for o in self._emit_ops[name]:
            if o[0] == "wait":
                eng.wait_ge(o[1], o[2])
            else:
                o[1](eng).then_inc(o[2], o[3])

    def emit(self):
        ops, self.ops = self.ops, {e: [] for e in ENGS}
        self._emit_ops = ops
        with self.nc.Block() as block:
            @block.tensor
            def _(e):
                self.replay("pe", e)

            @block.scalar
            def _(e):
                self.replay("act", e)

            @block.vector
            def _(e):
                self.replay("dve", e)

            @block.gpsimd
            def _(e):
                self.replay("pool", e)

            @block.sync
            def _(e):
                self.replay("sp", e)


def lambda_init(layer):
    return 0.8 - 0.6 * math.exp(-0.3 * layer)


B_WIN = (1, 2, 8)
B_MI0 = (0, 3, 8)


def c_deltas(j):
    if j == 0:
        return [-2, -1, 0, 1, 2, 3]
    if j == NT - 1:
        return [-3, -2, -1, 0, 1, 2]
    return [-2, -1, 0, 1, 2]


def c_type(j):
    return {0: 1, 1: 2, NT - 2: 3, NT - 1: 4}.get(j, 0)


def build_program(dbg=False, layers=DEPTH, phases="PABCM"):
    nc = bass.Bass("TRN2", target_bir_lowering=False, num_devices=NCORES)

    def din(name, shape, dt=F32):
        return nc.dram_tensor(name, shape, dt, kind="ExternalInput").ap()

    def dscr(name, shape, dt):
        ext = dbg and name in ("x1_d",)
        return nc.dram_tensor(name, shape, dt, kind="ExternalOutput" if ext else "Internal").ap()

    x_in = din("x", [S_OWN, D])
    c_in = din("c", [1, D])
    pos_in = din("pos", [1, S_OWN], I32)
    ng_in = din("norm_gain", [DEPTH, D])
    wada_in = din("w_ada", [DEPTH, D, 3 * D])
    bada_in = din("b_ada", [DEPTH, 3 * D])
    wp_in = din("wp", [DEPTH, D, WP_COLS])
    wg_in = din("wg", [DEPTH, D, 3 * D])
    dl_in = din("diff_lambda", [DEPTH, 256])
    sg_in = din("subln", [DEPTH, 128])
    rpb_in = din("rpbg", [DEPTH, 128, 7 * 4 * 128])
    wb_in = din("w_branch", [DEPTH, D, D])
    wo_in = din("w_out", [DEPTH, D, D])
    fg_in = din("final_gain", [1, D])
    ident_in = din("ident", [128, 128])
    invc_in = din("inv_col", [128, 1])
    sgnc_in = din("sign_col", [128, 1])
    mbb_in = din("mbb", [128, 25 * 128])
    mc_in = din("mc", [128, 5 * 7 * 128])
    kv_in = din("kv", [128, NWT])
    out_d = nc.dram_tensor("out", [S_OWN, D], F32, kind="ExternalOutput").ap()

    cf_d = dscr("cf_d", [128, S_OWN], F32)
    sf_d = dscr("sf_d", [128, S_OWN], F32)
    hT_d = dscr("hT_d", [128, 8, S_OWN], BF16)
    qkT_d = dscr("qkT_d", [24, 128, S_OWN], BF16)
    vx_d = dscr("vx_d", [S_OWN, VX_COLS], BF16)
    zs_d = dscr("zs_d", [S_OWN, D], F32)
    yz_d = dscr("yz_d", [S_OWN, D], BF16)
    x1_d = dscr("x1_d", [S_OWN, D], F32)
    ksh_d = [nc.dram_tensor("ksh%d" % l, [2, 12 * 128, S_OWN], BF16, kind="Internal", addr_space="Shared").ap()
             for l in range(DEPTH)]
    vsh_d = [nc.dram_tensor("vsh%d" % l, [2, S_OWN, VX_COLS], BF16, kind="Internal", addr_space="Shared").ap()
             for l in range(DEPTH)]
    ada_d = dscr("ada_d", [DEPTH, 3 * D], F32)

    with ExitStack() as top:
        S = Sched(nc, top)
        DIN = S.buf("din", dram=True)
        B_tab = S.buf("tab_d", dram=True)
        B_hT = S.buf("hT_d", dram=True)
        B_qk = S.buf("qkT_d", dram=True)
        B_vx = S.buf("vx_d", dram=True)
        B_zs = S.buf("zs_d", dram=True)
        B_yz = S.buf("yz_d", dram=True)
        B_x1 = S.buf("x1_d", dram=True)
        B_ada = S.buf("ada_d", dram=True)
        B_out = S.buf("out_d", dram=True)
        B_ksh = [S.buf("ksh%d" % l, dram=True) for l in range(DEPTH)]
        B_vsh = [S.buf("vsh%d" % l, dram=True) for l in range(DEPTH)]
        _par = {}

        def PAR(e, other=False):
            k = id(e)
            if k not in _par:
                p = e.partition_id() % 2
                _par[k] = (p, 1 - p)
            return _par[k][1 if other else 0]

        uid = [0]

        def alloc(st, name, shape, dt):
            uid[0] += 1
            nm = "%s_%d" % (name, uid[0])
            return st.enter_context(nc.sbuf_tensor(nm, shape, dt)), S.buf(nm)

        def palloc(st, name, shape, dt):
            uid[0] += 1
            nm = "%s_%d" % (name, uid[0])
            return st.enter_context(nc.psum_tensor(nm, shape, dt)), S.buf(nm)

        def MM(out, lhsT, rhs, start, stop, R, W, skip=False):
            if skip:
                S.op("pe", lambda e: e.matmul(out, lhsT=lhsT, rhs=rhs, start=start, stop=stop,
                                              skip_group_check=True), R, W)
            else:
                S.op("pe", lambda e: e.matmul(out, lhsT=lhsT, rhs=rhs, start=start, stop=stop), R, W)

        def TR(out, in_, ident, R, W):
            S.op("pe", lambda e: e.transpose(out, in_, ident), R, W)

        def ACT(out, in_, func, R, W, scale=None, bias=None, accum=None):
            kw = {}
            if scale is not None:
                kw["scale"] = scale
            if bias is not None:
                kw["bias"] = bias
            if accum is not None:
                kw["accum_out"] = accum
            S.op("act", lambda e: e.activation(out=out, in_=in_, func=func, **kw), R, W)

        def TS(eng, out, in0, s1, s2, op0, op1, R, W):
            if op1 is None:
                S.op(eng, lambda e: e.tensor_scalar(out=out, in0=in0, scalar1=s1, scalar2=None, op0=op0), R, W)
            else:
                S.op(eng, lambda e: e.tensor_scalar(out=out, in0=in0, scalar1=s1, scalar2=s2, op0=op0, op1=op1), R, W)

        def TT(eng, out, in0, in1, op, R, W):
            S.op(eng, lambda e: e.tensor_tensor(out=out, in0=in0, in1=in1, op=op), R, W)

        def STT(eng, out, in0, scalar, in1, op0, op1, R, W):
            S.op(eng, lambda e: e.scalar_tensor_tensor(out=out, in0=in0, scalar=scalar, in1=in1,
                                                       op0=op0, op1=op1), R, W)

        def CP(eng, out, in_, R, W):
            S.op(eng, lambda e: e.tensor_copy(out=out, in_=in_), R, W)

        def RCP(out, in_, R, W):
            S.op("dve", lambda e: e.reciprocal(out=out, in_=in_), R, W)

        def MSET(eng, ap, val, W):
            S.op(eng, lambda e: e.memset(ap, val), [], W)

        def DMA(eng, out, in_, src, dst, sembuf=None, slow=False):
            if slow:
                S.dma(eng, lambda e: e.dma_start(out=out, in_=in_, allow_slow_non_contiguous=True), src, dst, sembuf)
            else:
                S.dma(eng, lambda e: e.dma_start(out=out, in_=in_), src, dst, sembuf)

        def rstd_ops(st_tag, ss, B_ss, rs, B_rs, nhalf, B_nh, n):
            TS("pool", rs, ss, 1.0 / n, EPS, ALU.mult, ALU.add, [B_ss], [B_rs])
            TT("pool", rs, rs, nhalf, ALU.pow, [B_rs, B_nh], [B_rs])

        ident, B_id = alloc(top, "ident", [128, 128], BF16)
        nhalf, B_nh = alloc(top, "nhalf", [128, 1], F32)
        DMA("pool", ident[:], ident_in, DIN, B_id)
        MSET("pool", nhalf[:], -0.5, [B_nh])

        with ExitStack() as ph:
            CH = 2048
            invc, B_invc = alloc(ph, "invc", [128, 1], F32)
            sgnc, B_sgnc = alloc(ph, "sgnc", [128, 1], F32)
            posi, B_posi = alloc(ph, "posi", [128, CH], I32)
            posf, B_posf = alloc(ph, "posf", [128, CH], F32)
            ang, B_ang = alloc(ph, "ang", [128, CH], F32)
            ki, B_ki = alloc(ph, "ki", [128, CH], I32)
            kf, B_kf = alloc(ph, "kf", [128, CH], F32)
            m1, B_m1 = alloc(ph, "m1", [128, CH], F32)
            tab, B_tb = alloc(ph, "tab", [128, CH], F32)
            DMA("sp", invc[:], invc_in, DIN, B_invc)
            DMA("sp", sgnc[:], sgnc_in, DIN, B_sgnc)
            for ch in range(S_OWN // CH):
                DMA("sp", posi[:], pos_in[:, ch * CH:(ch + 1) * CH].partition_broadcast(128), DIN, B_posi)
                CP("dve", posf[:], posi[:], [B_posi], [B_posf])
                for which in range(2):
                    TS("dve", ang[:], posf[:], invc[:, 0:1], (math.pi / 2 if which == 0 else 0.0),
                       ALU.mult, ALU.add, [B_posf, B_invc], [B_ang])
                    TS("dve", ki[:], ang[:], 1.0 / TWO_PI, None, ALU.mult, None, [B_ang], [B_ki])
                    CP("dve", kf[:], ki[:], [B_ki], [B_kf])
                    STT("dve", ang[:], kf[:], -TWO_PI, ang[:], ALU.mult, ALU.add, [B_kf, B_ang], [B_ang])
                    TS("dve", m1[:], ang[:], -math.pi, TWO_PI, ALU.is_lt, ALU.mult, [B_ang], [B_m1])
                    TT("dve", ang[:], ang[:], m1[:], ALU.add, [B_ang, B_m1], [B_ang])
                    TS("dve", m1[:], ang[:], math.pi, -TWO_PI, ALU.is_gt, ALU.mult, [B_ang], [B_m1])
                    TT("dve", ang[:], ang[:], m1[:], ALU.add, [B_ang, B_m1], [B_ang])
                    if which == 0:
                        ACT(tab[:], ang[:], AF.Sin, [B_ang], [B_tb])
                        DMA("sp", cf_d[:, ch * CH:(ch + 1) * CH], tab[:], B_tb, B_tab)
                    else:
                        ACT(tab[:], ang[:], AF.Sin, [B_ang, B_sgnc], [B_tb], scale=sgnc[:, 0:1])
                        DMA("sp", sf_d[:, ch * CH:(ch + 1) * CH], tab[:], B_tb, B_tab)
            S.end_phase()

        def run_pipeline(events, la):
            n = len(events)
            for i in range(n + la):
                if i < n:
                    ev = events[i]
                    for f in ev.get("pre", ()):
                        f()
                    ev["s1"]()
                    ev["s2"]()
                k = i - la
                if k >= 0:
                    ev = events[k]
                    ev["s3"]()
                    for f in ev.get("post", ()):
                        f()

        def phase_P(l):
            x_src, B_xs = (x_in, DIN) if l == 0 else (x1_d, B_x1)
            with ExitStack() as ph:
                hT, B_h = alloc(ph, "hT", [128, 8, SBT], BF16)
                Cf, B_cf = alloc(ph, "Cf", [128, SBT], F32)
                Sf, B_sf = alloc(ph, "Sf", [128, SBT], F32)
                wm = [alloc(ph, "wm%d" % i, [128, 8, 512], BF16) for i in range(2)]
                ws = [alloc(ph, "ws%d" % i, [128, 8, 512], BF16) for i in range(2)]
                wv, B_wv = alloc(ph, "wv", [128, 8, 1536], BF16)
                xt = [alloc(ph, "xt%d" % i, [128, D], F32) for i in range(2)]
                tmpn = [alloc(ph, "tmpn%d" % i, [128, D], F32) for i in range(2)]
                ht = [alloc(ph, "ht%d" % i, [128, D], BF16) for i in range(2)]
                junk, B_junk = alloc(ph, "junk", [128, D], BF16)
                ss = [alloc(ph, "ss%d" % i, [128, 1], F32) for i in range(2)]
                rs = [alloc(ph, "rs%d" % i, [128, 1], F32) for i in range(2)]
                adab, B_adab = alloc(ph, "adab", [128, 3 * D], F32)
                gs, B_gs = alloc(ph, "gs", [128, D], F32)
                cT32, B_cT32 = alloc(ph, "cT32", [128, 8], F32)
                cT, B_cT = alloc(ph, "cT", [128, 8], BF16)
                arow = [alloc(ph, "arow%d" % i, [1, 512], F32) for i in range(2)]
                brow = [alloc(ph, "brow%d" % i, [1, 512], F32) for i in range(2)]
                t1 = [alloc(ph, "t1_%d" % i, [128, 512], F32) for i in range(2)]
                t2 = [alloc(ph, "t2_%d" % i, [128, 512], F32) for i in range(2)]
                ob = [alloc(ph, "ob%d" % i, [128, 512], BF16) for i in range(3)]
                vst = [alloc(ph, "vst%d" % i, [128, VX_COLS], BF16) for i in range(2)]
                zst = [alloc(ph, "zst%d" % i, [128, 512], F32) for i in range(2)]
                pm = [palloc(ph, "pm%d" % i, [128, 512], F32) for i in range(2)]
                psw = [palloc(ph, "psw%d" % i, [128, 512], F32) for i in range(2)]
                pv = [palloc(ph, "pv%d" % i, [128, 512], F32) for i in range(2)]
                ptr, B_ptr = palloc(ph, "ptr", [128, D], BF16)
                prow, B_prow = palloc(ph, "prow", [128, 512], F32)

                DMA("sp", cT32[:], c_in.rearrange("o (c p) -> p (o c)", p=128), DIN, B_cT32, slow=True)
                ACT(cT[:], cT32[:], AF.Silu, [B_cT32], [B_cT])
                wav = wada_in[l].rearrange("(c p) n -> p c n", p=128)
                for nb in range(6):
                    w_t, B_w = wm[nb % 2]
                    DMA("pool", w_t[:], wav[:, :, nb * 512:(nb + 1) * 512], DIN, B_w)
                    for c in range(8):
                        MM(prow[0:1, :], cT[:, c:c + 1], w_t[:, c, :], c == 0, c == 7, [B_cT, B_w], [B_prow])
                    a_r, B_ar = arow[nb % 2]
                    b_r, B_br = brow[nb % 2]
                    DMA("sp", b_r[:], bada_in[l:l + 1, nb * 512:(nb + 1) * 512], DIN, B_br)
                    TT("dve", a_r[:], prow[0:1, :], b_r[:], ALU.add, [B_prow, B_br], [B_ar])
                    DMA("sp", ada_d[l:l + 1, nb * 512:(nb + 1) * 512], a_r[:], B_ar, B_ada)
                DMA("sp", adab[:], ada_d[l:l + 1, :].partition_broadcast(128), B_ada, B_adab)
                DMA("sp", gs[:], ng_in[l:l + 1, :].partition_broadcast(128), DIN, B_gs)
                STT("dve", gs[:], adab[:, D:2 * D], 1.0, gs[:], ALU.add, ALU.mult, [B_adab, B_gs], [B_gs])
                for v_t, B_v in vst:
                    MSET("pool", v_t[:], 1.0, [B_v])

                wpv = wp_in[l].rearrange("(c p) n -> p c n", p=128)
                for sbk in range(NSB):
                    t0 = sbk * SBT
                    DMA("sp", Cf[:], cf_d[:, t0:t0 + SBT], B_tab, B_cf)
                    DMA("sp", Sf[:], sf_d[:, t0:t0 + SBT], B_tab, B_sf)
                    for t in range(SBT // 128):
                        i = t % 2
                        x_t, B_x = xt[i]
                        tm, B_tm = tmpn[i]
                        h_t, B_ht = ht[i]
                        s_t, B_s = ss[i]
                        r_t, B_r = rs[i]
                        tok = t0 + t * 128
                        DMA("sp", x_t[:], x_src[tok:tok + 128, :], B_xs, B_x)
                        MSET("pool", s_t[:], 0.0, [B_s])
                        ACT(junk[:], x_t[:], AF.Square, [B_x], [B_junk, B_s], accum=s_t[:])
                        rstd_ops("n", s_t[:], B_s, r_t[:], B_r, nhalf[:], B_nh, D)
                        STT("dve", tm[:], x_t[:], r_t[:, 0:1], gs[:], ALU.mult, ALU.mult, [B_x, B_r, B_gs], [B_tm])
                        TT("pool", h_t[:], tm[:], adab[:, 0:D], ALU.add, [B_tm, B_adab], [B_ht])
                        for c in range(8):
                            TR(ptr[:, c * 128:(c + 1) * 128], h_t[:, c * 128:(c + 1) * 128], ident[:], [B_ht, B_id], [B_ptr])
                        ACT(hT[:, :, t * 128:(t + 1) * 128], ptr[:].rearrange("p (c t) -> p c t", c=8), AF.Copy,
                            [B_ptr], [B_h])
                    DMA("act", hT_d[:, :, t0:t0 + SBT], hT[:], B_h, B_hT)

                    groups = [(0, 1024, [0, 1, 2, 3]), (512, 1536, [12, 13, 14, 15]),
                              (2048, 3584, [4, 5, 6, 7]), (2560, 4096, [8, 9, 16, 17]), (3072, 4608, [18, 19, 20, 21]),
                              (5120, None, [10, 11, 22, 23])]
                    for gi, (cb, sbase, chl) in enumerate(groups):
                        nch = len(chl)
                        w_t, B_w = wm[gi % 2]
                        DMA("pool", w_t[:], wpv[:, :, cb:cb + 512], DIN, B_w)
                        if sbase is not None:
                            s_w, B_sw = ws[gi % 2]
                            DMA("pool", s_w[:], wpv[:, :, sbase:sbase + 512], DIN, B_sw)
                        for ci in range(nch):
                            for tb in range(SBT // 512):
                                k = (ci * 4 + tb) % 2
                                p_m, B_pm = pm[k]
                                for c in range(8):
                                    MM(p_m[:], w_t[:, c, ci * 128:(ci + 1) * 128], hT[:, c, tb * 512:(tb + 1) * 512],
                                       c == 0, c == 7, [B_w, B_h], [B_pm])
                                o_t, B_o = ob[(ci * 4 + tb) % 3]
                                dst = qkT_d[chl[ci], :, t0 + tb * 512:t0 + (tb + 1) * 512]
                                if sbase is not None:
                                    p_s, B_ps = psw[k]
                                    for c in range(8):
                                        MM(p_s[:], s_w[:, c, ci * 128:(ci + 1) * 128], hT[:, c, tb * 512:(tb + 1) * 512],
                                           c == 0, c == 7, [B_sw, B_h], [B_ps])
                                    a_t, B_a = t1[k]
                                    b_t, B_b = t2[k]
                                    TT("dve", a_t[:], p_m[:], Cf[:, tb * 512:(tb + 1) * 512], ALU.mult, [B_pm, B_cf], [B_a])
                                    TT("dve", b_t[:], p_s[:], Sf[:, tb * 512:(tb + 1) * 512], ALU.mult, [B_ps, B_sf], [B_b])
                                    TT("pool", o_t[:], a_t[:], b_t[:], ALU.add, [B_a, B_b], [B_o])
                                    DMA("pool", dst, o_t[:], B_o, B_qk)
                                else:
                                    ACT(o_t[:], p_m[:], AF.Copy, [B_pm], [B_o])
                                    DMA("act", dst, o_t[:], B_o, B_qk)

                    for vb in range(3):
                        DMA("pool", wv[:, :, vb * 512:(vb + 1) * 512], wpv[:, :, 5632 + vb * 512:5632 + (vb + 1) * 512],
                            DIN, B_wv)
                    for t in range(SBT // 128):
                        v_t, B_v = vst[t % 2]
                        tok = t0 + t * 128
                        for vb in range(3):
                            p_v, B_pv = pv[vb % 2]
                            for c in range(8):
                                MM(p_v[:], hT[:, c, t * 128:(t + 1) * 128], wv[:, c, vb * 512:(vb + 1) * 512],
                                   c == 0, c == 7, [B_h, B_wv], [B_pv])
                            if vb == 0:
                                CP("dve", v_t[:, 0:516].rearrange("p (h e) -> p h e", e=129)[:, :, 0:128],
                                   p_v[:].rearrange("p (h e) -> p h e", e=128), [B_pv], [B_v])
                            else:
                                o0 = 516 + (vb - 1) * 520
                                CP("dve", v_t[:, o0:o0 + 520].rearrange("p (h e) -> p h e", e=65)[:, :, 0:64],
                                   p_v[:].rearrange("p (h e) -> p h e", e=64), [B_pv], [B_v])
                        DMA("pool", vx_d[tok:tok + 128, :], v_t[:], B_v, B_vx)
                    for zb in range(2):
                        w_t, B_w = wm[zb % 2]
                        DMA("pool", w_t[:], wpv[:, :, 7168 + zb * 512:7168 + (zb + 1) * 512], DIN, B_w)
                        for t in range(SBT // 128):
                            z_t, B_z = zst[t % 2]
                            p_v, B_pv = pv[t % 2]
                            tok = t0 + t * 128
                            for c in range(8):
                                MM(p_v[:], hT[:, c, t * 128:(t + 1) * 128], w_t[:, c, :], c == 0, c == 7, [B_h, B_w], [B_pv])
                            ACT(z_t[:], p_v[:], AF.Silu, [B_pv], [B_z])
                            DMA("act", zs_d[tok:tok + 128, zb * 512:(zb + 1) * 512], z_t[:], B_z, B_zs)
                cpb = S.buf("cpb")
                for q2 in range(2):
                    S.dma("sp", (lambda q2: (lambda e: e.dma_start(
                        out=ksh_d[l][bass.ds(PAR(e), 1), q2 * 768:(q2 + 1) * 768, :].rearrange("o n t -> (o n) t"),
                        in_=qkT_d[12 + q2 * 6:18 + q2 * 6, :, :].rearrange("c p t -> (c p) t"))))(q2),
                        B_qk, B_ksh[l], cpb)
                    S.dma("sp", (lambda q2: (lambda e: e.dma_start(
                        out=vsh_d[l][bass.ds(PAR(e), 1), q2 * 1024:(q2 + 1) * 1024, :].rearrange("o n c -> (o n) c"),
                        in_=vx_d[q2 * 1024:(q2 + 1) * 1024, :])))(q2),
                        B_vx, B_vsh[l], cpb)
                S.end_phase()
            S.emit()
            nc.all_core_barrier()

        def phase_A(l):
            li = lambda_init(l)
            with ExitStack() as ph:
                dl, B_dl = alloc(ph, "dl", [128, 256], F32)
                pr, B_pr = alloc(ph, "pr", [128, 128], F32)
                s12, B_s12 = alloc(ph, "s12", [128, 2], F32)
                e12, B_e12 = alloc(ph, "e12", [128, 2], F32)
                nlam, B_nlam = alloc(ph, "nlam", [128, 1], F32)
                sgb, B_sgb = alloc(ph, "sgb", [128, 128], F32)
                KT = [alloc(ph, "KT%d" % i, [128, S_LEN], BF16) for i in range(2)]
                VX = [alloc(ph, "VX%d" % i, [128, NTK, 129], BF16) for i in range(2)]
                QT = [alloc(ph, "QT%d" % i, [128, 512], BF16) for i in range(2)]
                ZS = [alloc(ph, "ZS%d" % i, [128, 4, 128], F32) for i in range(2)]
                E = [alloc(ph, "E%d" % i, [128, 512], BF16) for i in range(4)]
                st = [palloc(ph, "st%d" % i, [128, 512], F32) for i in range(4)]
                acc = [palloc(ph, "acc%d" % i, [128, 512], F32) for i in range(3)]
                rd, B_rd = alloc(ph, "rd", [128, 2], F32)
                o0, B_o0 = alloc(ph, "o0", [128, 128], F32)
                o1, B_o1 = alloc(ph, "o1", [128, 128], F32)
                oo, B_oo = alloc(ph, "oo", [128, 128], F32)
                jk, B_jk = alloc(ph, "jk", [128, 128], BF16)
                ssq, B_ssq = alloc(ph, "ssq", [128, 1], F32)
                rsq, B_rsq = alloc(ph, "rsq", [128, 1], F32)
                yy, B_yy = alloc(ph, "yy", [128, 128], F32)
                yzt = [alloc(ph, "yzt%d" % i, [128, 4, 128], BF16) for i in range(2)]

                DMA("sp", dl[:], dl_in[l:l + 1, :].partition_broadcast(128), DIN, B_dl)
                dlv = dl[:].rearrange("p (a b d) -> p a b d", a=2, b=2)
                TT("dve", pr[:].rearrange("p (a d) -> p a d", a=2), dlv[:, :, 0, :], dlv[:, :, 1, :], ALU.mult, [B_dl], [B_pr])
                S.op("dve", lambda e: e.reduce_sum(out=s12[:], in_=pr[:].rearrange("p (a d) -> p a d", a=2),
                                                   axis=mybir.AxisListType.X), [B_pr], [B_s12])
                ACT(e12[:], s12[:], AF.Exp, [B_s12], [B_e12])
                STT("dve", nlam[:], e12[:, 1:2], -li, e12[:, 0:1], ALU.add, ALU.subtract, [B_e12], [B_nlam])
                DMA("sp", sgb[:], sg_in[l:l + 1, :].partition_broadcast(128), DIN, B_sgb)
                TS("dve", sgb[:], sgb[:], 1.0 - li, None, ALU.mult, None, [B_sgb], [B_sgb])

                def accreg(comp, qi):
                    r = comp * 4 + qi
                    return acc[r // 3][0], acc[r // 3][1], (r % 3) * 130, (r % 3 == 0)

                accs = [alloc(ph, "accs%d" % i, [128, 3, 390], F32) for i in range(2)]
                events = []

                def mk_head_pre(hd):
                    def f():
                        K_t, B_K = KT[hd % 2]
                        V_t, B_V = VX[hd % 2]
                        for r in range(2):
                            DMA("sp", K_t[:, r * S_OWN:(r + 1) * S_OWN],
                                ksh_d[l][r, hd * 128:(hd + 1) * 128, :], B_ksh[l], B_K)
                            vsrc = vsh_d[l][r, :, hd * 129:(hd + 1) * 129].rearrange("(kt p) e -> p kt e", p=128)
                            for q2 in range(2):
                                DMA("sp", V_t[:, r * NT + q2 * 8:r * NT + (q2 + 1) * 8, :], vsrc[:, q2 * 8:(q2 + 1) * 8, :],
                                    B_vsh[l], B_V)
                    return f

                def mk_qb_pre(hd, qb):
                    def f():
                        j = (hd * NQB + qb) % 2
                        q0 = qb * 512
                        DMA("sp", QT[j][0][:], qkT_d[hd, :, q0:q0 + 512], B_qk, QT[j][1])
                        DMA("sp", ZS[j][0][:],
                            zs_d[q0:q0 + 512, hd * 128:(hd + 1) * 128].rearrange("(qi p) e -> p qi e", p=128),
                            B_zs, ZS[j][1])
                    return f

                def mk_event(idx, hd, qb, kc):
                    K_t, B_K = KT[hd % 2]
                    V_t, B_V = VX[hd % 2]
                    Q_t, B_Q = QT[(hd * NQB + qb) % 2]
                    banks = [(2 * idx + c) % 4 for c in range(2)]

                    def s1():
                        for comp in range(2):
                            s_t, B_st = st[banks[comp]]
                            MM(s_t[:], K_t[comp * 64:(comp + 1) * 64, kc * 128:(kc + 1) * 128],
                               Q_t[comp * 64:(comp + 1) * 64, :], True, True, [B_K, B_Q], [B_st])

                    def s2():
                        for comp in range(2):
                            s_t, B_st = st[banks[comp]]
                            e_t, B_e = E[banks[comp]]
                            ACT(e_t[:], s_t[:], AF.Exp, [B_st], [B_e], scale=0.125)

                    def s3():
                        for comp in range(2):
                            e_t, B_e = E[banks[comp]]
                            for qi in range(4):
                                a_t, B_a, off, first = accreg(comp, qi)
                                MM(a_t[:, off:off + 129], e_t[:, qi * 128:(qi + 1) * 128], V_t[:, kc, :],
                                   (kc == 0 and first), kc == NTK - 1, [B_e, B_V], [B_a], skip=True)
                    return {"s1": s1, "s2": s2, "s3": s3}

                def mk_finalize(hd, qb):
                    def f():
                        j = (hd * NQB + qb) % 2
                        Z_t, B_Z = ZS[j]
                        y_t, B_y = yzt[j]
                        sa, B_sa = accs[j]
                        q0 = qb * 512
                        for bk in range(3):
                            CP("dve", sa[:, bk, 0:389], acc[bk][0][:, 0:389], [acc[bk][1]], [B_sa])
                        for qi in range(4):
                            r0_, r1_ = qi, 4 + qi
                            a0 = sa[:, r0_ // 3, (r0_ % 3) * 130:(r0_ % 3) * 130 + 129]
                            a1 = sa[:, r1_ // 3, (r1_ % 3) * 130:(r1_ % 3) * 130 + 129]
                            RCP(rd[:, 0:1], a0[:, 128:129], [B_sa], [B_rd])
                            RCP(rd[:, 1:2], a1[:, 128:129], [B_sa, B_rd], [B_rd])
                            TS("dve", o0[:], a0[:, 0:128], rd[:, 0:1], None, ALU.mult, None, [B_sa, B_rd], [B_o0])
                            TS("dve", o1[:], a1[:, 0:128], rd[:, 1:2], None, ALU.mult, None, [B_sa, B_rd], [B_o1])
                            STT("dve", oo[:], o1[:], nlam[:, 0:1], o0[:], ALU.mult, ALU.add, [B_o1, B_o0, B_nlam], [B_oo])
                            MSET("pool", ssq[:], 0.0, [B_ssq])
                            ACT(jk[:], oo[:], AF.Square, [B_oo], [B_jk, B_ssq], accum=ssq[:])
                            rstd_ops("a", ssq[:], B_ssq, rsq[:], B_rsq, nhalf[:], B_nh, 128)
                            STT("dve", yy[:], oo[:], rsq[:, 0:1], sgb[:], ALU.mult, ALU.mult, [B_oo, B_rsq, B_sgb], [B_yy])
                            TT("pool", y_t[:, qi, :], yy[:], Z_t[:, qi, :], ALU.mult, [B_yy, B_Z], [B_y])
                        DMA("pool", yz_d[q0:q0 + 512, hd * 128:(hd + 1) * 128].rearrange("(qi p) e -> p qi e", p=128),
                            y_t[:], B_y, B_yz)
                    return f

                idx = 0
                for hd in range(4):
                    for qb in range(NQB):
                        for kc in range(NTK):
                            ev = mk_event(idx, hd, qb, kc)
                            pre = []
                            if qb == 0 and kc == 0:
                                pre.append(mk_head_pre(hd))
                            if kc == 0:
                                pre.append(mk_qb_pre(hd, qb))
                            ev["pre"] = pre
                            if kc == NTK - 1:
                                ev["post"] = [mk_finalize(hd, qb)]
                            events.append(ev)
                            idx += 1
                run_pipeline(events, 1)
                S.end_phase()

        def phase_BC(l, which):
            with ExitStack() as ph:
                KT, B_K = alloc(ph, "KTb", [128, 2, S_OWN], BF16)
                QT, B_Q = alloc(ph, "QTb", [128, 2, S_OWN], BF16)
                VX, B_V = alloc(ph, "VXb", [128, NT, 260], BF16)
                nhc = 6 if which == "B" else 2
                hcols = 780 if which == "B" else 260
                hki0 = 4 if which == "B" else 10
                hvc0 = 516 if which == "B" else 516 + 780
                H = S_OWN // 2
                KH, B_KH = alloc(ph, "KH", [128, nhc, 2, H], BF16)
                VH, B_VH = alloc(ph, "VH", [128, 2, 8, hcols], BF16)
                kvb, B_kv = alloc(ph, "kvb", [128, NWT], F32)
                DMA("sp", kvb[:], kv_in, DIN, B_kv)
                hq = "sp"
                for side in range(2):
                    t_lo = H if side == 0 else 0
                    DMA("sp", KH[:, :, side, :],
                        ksh_d[l][side, hki0 * 128:(hki0 + nhc) * 128, t_lo:t_lo + H].rearrange("(c p) t -> p c t", p=128),
                        B_ksh[l], B_KH)
                    for q2 in range(2):
                        DMA("sp", VH[:, side, q2 * 4:(q2 + 1) * 4, :],
                            vsh_d[l][side, t_lo + q2 * 512:t_lo + (q2 + 1) * 512, hvc0:hvc0 + hcols]
                            .rearrange("(kt p) e -> p kt e", p=128), B_vsh[l], B_VH)
                if which == "B":
                    mb, B_mb = alloc(ph, "mb", [128, 25, 128], F32)
                    num, B_num = alloc(ph, "num", [128, NT, 260], F32)
                    DMA("sp", mb[:], mbb_in.rearrange("p (m q) -> p m q", q=128), DIN, B_mb)
                else:
                    Gc, B_G = alloc(ph, "Gc", [128, 7, 4, 128], F32)
                    Mc, B_M = alloc(ph, "Mc", [128, 5, 7, 128], F32)
                    DMA("sp", Gc[:], rpb_in[l].rearrange("p (a h q) -> p a h q", a=7, h=4), DIN, B_G)
                    DMA("sp", Mc[:], mc_in.rearrange("p (t a q) -> p t a q", t=5, a=7), DIN, B_M)
                sbz = [alloc(ph, "sbz%d" % i, [128, 4, 128], F32) for i in range(2)]
                E = [alloc(ph, "Eb%d" % i, [128, 512], BF16) for i in range(3)]
                ZS = [alloc(ph, "ZSb%d" % i, [128, 256], F32) for i in range(2)]
                rd = [alloc(ph, "rdb%d" % i, [128, 4, 1], F32) for i in range(2)]
                yb = [alloc(ph, "yb%d" % i, [128, 4, 64], F32) for i in range(2)]
                yzt = [alloc(ph, "yztb%d" % i, [128, 256], BF16) for i in range(2)]
                st = [palloc(ph, "stb%d" % i, [128, 512], F32) for i in range(3)]
                sto = [palloc(ph, "sto%d" % i, [128, 512], F32) for i in range(3)]
                acc = [palloc(ph, "accb%d" % i, [128, 512], F32) for i in range(2)]
                col0 = 512 if which == "B" else 768

                def finalize(j, src, B_src):
                    i = j % 2
                    z_t, B_z = ZS[i]
                    r_t, B_r = rd[i]
                    y_t, B_y = yb[i]
                    o_t, B_o = yzt[i]
                    tok = j * 128
                    DMA("sp", z_t[:], zs_d[tok:tok + 128, col0:col0 + 256], B_zs, B_z)
                    sv = src.rearrange("p (h e) -> p h e", e=65)
                    RCP(r_t[:], sv[:, :, 64:65], [B_src], [B_r])
                    TT("dve", y_t[:], sv[:, :, 0:64], r_t[:].broadcast_to([128, 4, 64]), ALU.mult, [B_src, B_r], [B_y])
                    TT("pool", o_t[:], y_t[:].rearrange("p h e -> p (h e)"), z_t[:], ALU.mult, [B_y, B_z], [B_o])
                    DMA("pool", yz_d[tok:tok + 128, col0:col0 + 256], o_t[:], B_o, B_yz)

                ngroups = 3 if which == "B" else 1
                events = []

                def mk_group_pre(g):
                    def f():
                        if which == "B":
                            kch, qch, vcol = 16 + 2 * g, 4 + 2 * g, 516 + g * 260
                        else:
                            kch, qch, vcol = 22, 10, 516 + 780
                        for c2 in range(2):
                            DMA("sp", QT[:, c2, :], qkT_d[qch + c2, :, :], B_qk, B_Q)
                            DMA("sp", KT[:, c2, :], qkT_d[kch + c2, :, :], B_qk, B_K)
                        vsrc = vx_d[:, vcol:vcol + 260].rearrange("(kt p) e -> p kt e", p=128)
                        for q2 in range(2):
                            DMA("sp", VX[:, q2 * 8:(q2 + 1) * 8, :], vsrc[:, q2 * 8:(q2 + 1) * 8, :], B_vx, B_V)
                    return f

                def mk_event(idx, g, j, di, dlt, ndl):
                    kt = j + dlt + 8
                    s_t, B_st = st[idx % 3]
                    s_o, B_so = sto[idx % 3]
                    z_t, B_zb = sbz[idx % 2]
                    e_t, B_e = E[idx % 3]
                    a_t, B_a = acc[j % 2]

                    def s1():
                        for h in range(4):
                            r0 = (h % 2) * 64
                            dstb, B_dst = (s_t, B_st) if h % 2 == 0 else (s_o, B_so)
                            hh = h // 2
                            c2 = h // 2
                            if kt < 8:
                                kop, B_kop = KH[r0:r0 + 64, 2 * g + c2, 0, kt * 128:(kt + 1) * 128], B_KH
                            elif kt < 8 + NT:
                                kop, B_kop = KT[r0:r0 + 64, c2, (kt - 8) * 128:(kt - 7) * 128], B_K
                            else:
                                kop, B_kop = KH[r0:r0 + 64, 2 * g + c2, 1, (kt - 8 - NT) * 128:(kt - 7 - NT) * 128], B_KH
                            MM(dstb[:, hh * 128:(hh + 1) * 128], kop,
                               QT[r0:r0 + 64, c2, j * 128:(j + 1) * 128], True, True, [B_kop, B_Q], [B_dst], skip=True)

                    def s2():
                        for par, (srcb, B_srcb) in enumerate(((s_t, B_st), (s_o, B_so))):
                            sv = srcb[:, 0:256].rearrange("p (h q) -> p h q", h=2)
                            zv = z_t[:].rearrange("p (a b) q -> p a b q", b=2)[:, :, par, :]
                            if which == "B":
                                mi = B_MI0[g] + dlt + B_WIN[g]
                                STT("dve", zv, sv, 0.125, mb[:, mi, :].unsqueeze(1).broadcast_to([128, 2, 128]),
                                    ALU.mult, ALU.add, [B_srcb, B_mb], [B_zb])
                            else:
                                gv = Gc[:, dlt + 3, :, :].rearrange("p (a b) q -> p a b q", b=2)[:, :, par, :]
                                STT("dve", zv, sv, 0.125, gv, ALU.mult, ALU.add, [B_srcb, B_G], [B_zb])
                        if which == "C":
                            TT("pool", z_t[:], z_t[:], Mc[:, c_type(j), dlt + 3, :].unsqueeze(1).broadcast_to([128, 4, 128]),
                               ALU.add, [B_zb, B_M], [B_zb])
                        ACT(e_t[:], z_t[:].rearrange("p h q -> p (h q)"), AF.Exp, [B_zb, B_kv], [B_e],
                            bias=kvb[:, kt:kt + 1])

                    def s3():
                        for h in range(4):
                            if kt < 8:
                                vop, B_vop = VH[:, 0, kt, g * 260 + h * 65:g * 260 + (h + 1) * 65], B_VH
                            elif kt < 8 + NT:
                                vop, B_vop = VX[:, kt - 8, h * 65:(h + 1) * 65], B_V
                            else:
                                vop, B_vop = VH[:, 1, kt - 8 - NT, g * 260 + h * 65:g * 260 + (h + 1) * 65], B_VH
                            MM(a_t[:, h * 65:(h + 1) * 65], e_t[:, h * 128:(h + 1) * 128], vop,
                               (di == 0 and h == 0), di == ndl - 1, [B_e, B_vop], [B_a], skip=True)
                    return {"s1": s1, "s2": s2, "s3": s3}

                def mk_post(g, j):
                    def f():
                        a_t, B_a = acc[j % 2]
                        if which == "B":
                            if g == 0:
                                CP("dve", num[:, j, :], a_t[:, 0:260], [B_a], [B_num])
                            else:
                                TT("dve", num[:, j, :], num[:, j, :], a_t[:, 0:260], ALU.add, [B_a, B_num], [B_num])
                        else:
                            finalize(j, a_t[:, 0:260], B_a)
                    return f

                idx = 0
                for g in range(ngroups):
                    events = []
                    for j in range(NT):
                        if which == "B":
                            dl_list = list(range(-B_WIN[g], B_WIN[g] + 1))
                        else:
                            dl_list = c_deltas(j)
                        for di, dlt in enumerate(dl_list):
                            ev = mk_event(idx, g, j, di, dlt, len(dl_list))
                            if j == 0 and di == 0:
                                ev["pre"] = [mk_group_pre(g)]
                            if di == len(dl_list) - 1:
                                ev["post"] = [mk_post(g, j)]
                            events.append(ev)
                            idx += 1
                    run_pipeline(events, 2)
                if which == "B":
                    for j in range(NT):
                        finalize(j, num[:, j, :], B_num)
                S.end_phase()

        def phase_M(l):
            last = (l == DEPTH - 1)
            x_src, B_xs = (x_in, DIN) if l == 0 else (x1_d, B_x1)
            with ExitStack() as ph:
                Wg, B_Wg = alloc(ph, "Wg", [128, 8, 3 * D], BF16)
                Wb, B_Wb = alloc(ph, "Wb", [128, 8, D], BF16)
                Wo, B_Wo = alloc(ph, "Wo", [128, 8, D], BF16)
                gate, B_gate = alloc(ph, "gate", [128, D], F32)
                fgb, B_fgb = alloc(ph, "fgb", [128, D], F32)
                hTt = [alloc(ph, "hTt%d" % i, [128, 8, 512], BF16) for i in range(2)]
                yzl = [alloc(ph, "yzl%d" % i, [128, 4, D], BF16) for i in range(2)]
                yzT, B_yzT = alloc(ph, "yzT", [128, 8, 512], BF16)
                mT, B_mT = alloc(ph, "mT", [128, 8, 512], BF16)
                sg = [alloc(ph, "sg%d" % i, [128, 512], F32) for i in range(2)]
                tq = [alloc(ph, "tq%d" % i, [128, 512], F32) for i in range(3)]
                uq, B_uq = alloc(ph, "uq", [128, 512], F32)
                xt = [alloc(ph, "xtm%d" % i, [128, D], F32) for i in range(2)]
                xn = [alloc(ph, "xn%d" % i, [128, D], F32) for i in range(2)]
                to, B_to = alloc(ph, "to", [128, 512], F32)
                ssf, B_ssf = alloc(ph, "ssf", [128, 1], F32)
                rsf, B_rsf = alloc(ph, "rsf", [128, 1], F32)
                jkf, B_jkf = alloc(ph, "jkf", [128, D], BF16)
                ptr, B_ptr = palloc(ph, "ptrm", [128, D], BF16)
                pb = [palloc(ph, "pb%d" % i, [128, 512], F32) for i in range(2)]
                pg = [palloc(ph, "pg%d" % i, [128, 512], F32) for i in range(2)]
                po = [palloc(ph, "po%d" % i, [128, 512], F32) for i in range(2)]

                wgv = wg_in[l].rearrange("(c p) n -> p c n", p=128)
                for nb in range(6):
                    DMA("pool", Wg[:, :, nb * 512:(nb + 1) * 512], wgv[:, :, nb * 512:(nb + 1) * 512], DIN, B_Wg)
                wbv = wb_in[l].rearrange("(c p) n -> p c n", p=128)
                wov = wo_in[l].rearrange("(c p) n -> p c n", p=128)
                for nb in range(2):
                    DMA("pool", Wb[:, :, nb * 512:(nb + 1) * 512], wbv[:, :, nb * 512:(nb + 1) * 512], DIN, B_Wb)
                    DMA("pool", Wo[:, :, nb * 512:(nb + 1) * 512], wov[:, :, nb * 512:(nb + 1) * 512], DIN, B_Wo)
                DMA("sp", gate[:], ada_d[l:l + 1, 2 * D:3 * D].partition_broadcast(128), B_ada, B_gate)
                if last:
                    DMA("sp", fgb[:], fg_in.partition_broadcast(128), DIN, B_fgb)

                branches = [(0, 4), (4, 2), (6, 2)]
                it = 0
                for tb in range(S_OWN // 512):
                    h_t, B_h = hTt[tb % 2]
                    y_l, B_yl = yzl[tb % 2]
                    q0 = tb * 512
                    DMA("sp", h_t[:], hT_d[:, :, q0:q0 + 512], B_hT, B_h)
                    DMA("sp", y_l[:], yz_d[q0:q0 + 512, :].rearrange("(tt p) n -> p tt n", p=128), B_yz, B_yl)
                    for jc in range(8):
                        for tt in range(4):
                            TR(ptr[:, tt * 128:(tt + 1) * 128], y_l[:, tt, jc * 128:(jc + 1) * 128], ident[:],
                               [B_yl, B_id], [B_ptr])
                        CP("dve", yzT[:, jc, :], ptr[:, 0:512], [B_ptr], [B_yzT])
                    for ncx in range(8):
                        for bi, (jc0, njc) in enumerate(branches):
                            p_b, B_pb = pb[it % 2]
                            p_g, B_pg = pg[it % 2]
                            s_g, B_sg = sg[it % 2]
                            it += 1
                            for k in range(njc):
                                MM(p_b[:], Wb[:, jc0 + k, ncx * 128:(ncx + 1) * 128], yzT[:, jc0 + k, :],
                                   k == 0, k == njc - 1, [B_Wb, B_yzT], [B_pb])
                            for c in range(8):
                                MM(p_g[:], Wg[:, c, bi * D + ncx * 128:bi * D + (ncx + 1) * 128], h_t[:, c, :],
                                   c == 0, c == 7, [B_Wg, B_h], [B_pg])
                            ACT(s_g[:], p_g[:], AF.Sigmoid, [B_pg], [B_sg])
                            t_q, B_tq = tq[bi]
                            TT("dve", t_q[:], p_b[:], s_g[:], ALU.mult, [B_pb, B_sg], [B_tq])
                        TT("pool", uq[:], tq[0][0][:], tq[1][0][:], ALU.add, [tq[0][1], tq[1][1]], [B_uq])
                        TT("pool", mT[:, ncx, :], uq[:], tq[2][0][:], ALU.add, [B_uq, tq[2][1]], [B_mT])
                    for tt in range(4):
                        i = (tb * 4 + tt) % 2
                        x_t, B_x = xt[i]
                        x_n, B_xn = xn[i]
                        tok = q0 + tt * 128
                        DMA("sp", x_t[:], x_src[tok:tok + 128, :], B_xs, B_x)
                        for nb in range(2):
                            p_o, B_po = po[nb]
                            for jc in range(8):
                                MM(p_o[:], mT[:, jc, tt * 128:(tt + 1) * 128], Wo[:, jc, nb * 512:(nb + 1) * 512],
                                   jc == 0, jc == 7, [B_mT, B_Wo], [B_po])
                            TT("dve", to[:], p_o[:], gate[:, nb * 512:(nb + 1) * 512], ALU.mult, [B_po, B_gate], [B_to])
                            TT("pool", x_n[:, nb * 512:(nb + 1) * 512], to[:], x_t[:, nb * 512:(nb + 1) * 512], ALU.add,
                               [B_to, B_x], [B_xn])
                        if not last:
                            DMA("pool", x1_d[tok:tok + 128, :], x_n[:], B_xn, B_x1)
                        else:
                            o_f, B_of = x_t, B_x
                            MSET("pool", ssf[:], 0.0, [B_ssf])
                            ACT(jkf[:], x_n[:], AF.Square, [B_xn], [B_jkf, B_ssf], accum=ssf[:])
                            rstd_ops("f", ssf[:], B_ssf, rsf[:], B_rsf, nhalf[:], B_nh, D)
                            STT("dve", o_f[:], x_n[:], rsf[:, 0:1], fgb[:], ALU.mult, ALU.mult, [B_xn, B_rsf, B_fgb], [B_of])
                            DMA("pool", out_d[tok:tok + 128, :], o_f[:], B_of, B_out)
                S.end_phase()

        for l in range(layers):
            if "P" in phases:
                phase_P(l)
            if "A" in phases:
                phase_A(l)
            if "B" in phases:
                phase_BC(l, "B")
            if "C" in phases:
                phase_BC(l, "C")
            if "M" in phases:
                phase_M(l)
        S.barrier()
        S.emit()
        nc._sched_stats = {e: len(S.ops[e]) for e in ENGS}
    return nc


def _swap_cols(w):
    out = np.zeros_like(w)
    n = w.shape[-1]
    for base in range(0, n, 64):
        out[..., base:base + 8] = w[..., base + 8:base + 16]
        out[..., base + 8:base + 16] = w[..., base:base + 8]
    return out


def _const_tables():
    ident = np.eye(128, dtype=np.float32)
    inv = (np.float32(500000.0) ** (-np.arange(0, 16, 2, dtype=np.float32) / np.float32(16))).astype(np.float32)
    inv_col = np.zeros((128, 1), np.float32)
    sign_col = np.zeros((128, 1), np.float32)
    for r in range(128):
        i = r % 64
        if i < 16:
            inv_col[r, 0] = inv[i % 8]
            sign_col[r, 0] = -1.0 if i < 8 else 1.0
    kk = np.arange(128)[:, None]
    qq = np.arange(128)[None, :]
    mbb = np.zeros((128, 25, 128), np.float32)
    mi = 0
    for g, r in enumerate((1, 4, 16)):
        for dlt in range(-B_WIN[g], B_WIN[g] + 1):
            d = 128 * dlt + kk - qq
            valid = (d % r == 0) & (np.abs(d) <= 64 * r)
            mbb[:, mi, :] = np.where(valid, 0.0, NEG)
            mi += 1
    return ident, inv_col, sign_col, mbb.reshape(128, -1)


def _core_tables(hf):
    kk = np.arange(128)[:, None]
    qq = np.arange(128)[None, :]
    mc = np.zeros((128, 5, 7, 128), np.float32)
    for t, j in enumerate((10, NT * hf, NT * hf + 1, NT * hf + NT - 2, NT * hf + NT - 1)):
        for a, dlt in enumerate(range(-3, 4)):
            qrow = 2 * j + qq // 64
            qcol = qq % 64
            krow = 2 * (j + dlt) + kk // 64
            kcol = kk % 64
            rs = np.clip(qrow - 4, 0, 56)
            cs = np.clip(qcol - 8, 0, 48)
            valid = (krow >= rs) & (krow < rs + 8) & (kcol >= cs) & (kcol < cs + 16) & (krow >= 0) & (krow < 64)
            mc[:, t, a, :] = np.where(valid, 0.0, NEG)
    kv = np.zeros((128, NWT), np.float32)
    for w in range(NWT):
        gt = NT * hf - 8 + w
        if not (0 <= gt < NTK):
            kv[:, w] = NEG
    return mc.reshape(128, -1), kv


def _rpb_gather(na_rpb):
    kk = np.arange(128)[:, None]
    qq = np.arange(128)[None, :]
    L = na_rpb.shape[0]
    out = np.zeros((L, 128, 7, 4, 128), np.float32)
    for a, dlt in enumerate(range(-3, 4)):
        dr = np.clip(2 * dlt + kk // 64 - qq // 64 + 7, 0, 14)
        dc = np.clip(kk % 64 - qq % 64 + 15, 0, 30)
        for h in range(4):
            out[:, :, a, h, :] = na_rpb[:, h][:, dr, dc]
    return out.reshape(L, 128, -1)


def _prep_inputs(x, c, positions, norm_gain, w_ada, b_ada, w_in, diff_lambda, diff_subln_gain, na_rpb,
                 w_branch, w_out, final_gain):
    f = lambda a: np.ascontiguousarray(np.asarray(a, dtype=np.float32))
    w_in = f(w_in)
    qa = w_in[:, :, 0:1024]
    qb = w_in[:, :, 1536:3072]
    qc = np.concatenate([w_in[:, :, 3840:4096], w_in[:, :, 4096:4352]], axis=-1)
    v = np.concatenate([w_in[:, :, 1024:1536], w_in[:, :, 3072:3840], w_in[:, :, 4352:4608]], axis=-1)
    z = w_in[:, :, 4608:5632]
    wp = np.ascontiguousarray(np.concatenate([qa, _swap_cols(qa), qb, _swap_cols(qb), qc, v, z], axis=-1))
    assert wp.shape[-1] == WP_COLS
    wg = np.ascontiguousarray(w_in[:, :, 5632:8704])
    ident, inv_col, sign_col, mbb = _const_tables()
    shared = {
        "norm_gain": f(norm_gain), "w_ada": f(w_ada), "b_ada": f(b_ada), "wp": wp, "wg": wg,
        "diff_lambda": f(diff_lambda).reshape(DEPTH, 256), "subln": f(diff_subln_gain),
        "rpbg": _rpb_gather(f(na_rpb)), "w_branch": f(w_branch), "w_out": f(w_out),
        "final_gain": f(final_gain).reshape(1, D), "ident": ident, "inv_col": inv_col, "sign_col": sign_col,
        "mbb": mbb,
    }
    x = f(x)
    c = f(c)
    positions = np.ascontiguousarray(np.asarray(positions, dtype=np.int32))
    maps = []
    core_tabs = [_core_tables(hf) for hf in range(2)]
    for b in range(x.shape[0]):
        for hf in range(2):
            m = dict(shared)
            sl = slice(hf * S_OWN, (hf + 1) * S_OWN)
            m["x"] = np.ascontiguousarray(x[b, sl])
            m["c"] = np.ascontiguousarray(c[b:b + 1])
            m["pos"] = np.ascontiguousarray(positions[b:b + 1, sl])
            m["mc"], m["kv"] = core_tabs[hf]
            maps.append(m)
    return maps


_NC_CACHE = {}


def kernel(x, c, positions, norm_gain, w_ada, b_ada, w_in, diff_lambda, diff_subln_gain, na_rpb,
           w_branch, w_out, final_gain):
    maps = _prep_inputs(x, c, positions, norm_gain, w_ada, b_ada, w_in, diff_lambda, diff_subln_gain, na_rpb,
                        w_branch, w_out, final_gain)
    if "nc" not in _NC_CACHE:
        _NC_CACHE["nc"] = build_program()
    nc = _NC_CACHE["nc"]
    res = run_bass_kernel_spmd(nc, maps, core_ids=list(range(len(maps))))
    outs = [np.asarray(r["out"], dtype=np.float32) for r in res.results]
    nb = len(outs) // 2
    return np.stack([np.concatenate([outs[2 * b], outs[2 * b + 1]], axis=0) for b in range(nb)], axis=0)
```
